# Optimizing a Trainium2 kernel written in Bass

```python
import math
import jax, jax.numpy as jnp
from jax import lax
import numpy as np

D_MODEL = 1024
BATCH = 4
SEQ = 4096
DEPTH = 4

N_A = DEPTH // 2
N_B = DEPTH - N_A
POOL_WINDOWS = (2, 4, 8, 16)
POOL_GROUP = D_MODEL // len(POOL_WINDOWS)
N_HEADS = 16
HEAD_DIM = 64
N_KV_GROUPS = 4
HEADS_PER_GROUP = N_HEADS // N_KV_GROUPS
CMP_STRIDE = 16
CMP_LEN = 2 * CMP_STRIDE
CMP_HIDDEN = 128
SLC_BLOCK = 64
N_SELECT = 16
WINDOW = 512
Q_BLOCK = 64
N_BRANCH = 3
ROPE_THETA = 500000.0
ROT_DIM = HEAD_DIM // 4
D_FF = 2816
CONV_W = 3
PLE_DIM = 256

EPS = 1e-6
NEG = -1e30
FORCE_SCORE = 1e4

kernel_name = "hybrid_pool_nsa_yoco_trunk"


def rmsnorm(x, g):
    xf = x.astype(jnp.float32)
    y = xf * lax.rsqrt(jnp.mean(xf * xf, axis=-1, keepdims=True) + EPS) * g.astype(jnp.float32)
    return y.astype(x.dtype)


def rope_tables(pos):
    inv_freq = ROPE_THETA ** (-jnp.arange(0, ROT_DIM, 2, dtype=jnp.float32) / ROT_DIM)
    ang = pos.astype(jnp.float32)[..., None] * inv_freq
    return jnp.cos(ang), jnp.sin(ang)


def rope(x, cos, sin):
    xf = x.astype(jnp.float32)
    half = ROT_DIM // 2
    x1, x2, xp = xf[..., :half], xf[..., half:ROT_DIM], xf[..., ROT_DIM:]
    out = jnp.concatenate([x1 * cos - x2 * sin, x2 * cos + x1 * sin, xp], axis=-1)
    return out.astype(x.dtype)


def masked_softmax(s, mask):
    s = jnp.where(mask, s, NEG)
    m = jnp.max(s, axis=-1, keepdims=True)
    e = jnp.where(mask, jnp.exp(s - m), 0.0)
    return e / jnp.maximum(jnp.sum(e, axis=-1, keepdims=True), 1e-30)


def pool_mixer(h, w, scale):
    B, S, D = h.shape
    c = jnp.cumsum(h.astype(jnp.float32), axis=1)
    c = jnp.pad(c, ((0, 0), (1, 0), (0, 0)))
    t = jnp.arange(S)
    outs = []
    for gi, win in enumerate(POOL_WINDOWS):
        sl = slice(gi * POOL_GROUP, (gi + 1) * POOL_GROUP)
        cg = c[..., sl]
        lo = jnp.maximum(t + 1 - win, 0)
        cnt = jnp.minimum(t + 1, win).astype(jnp.float32)
        avg = (cg[:, t + 1] - cg[:, lo]) / cnt[:, None]
        u = (avg - h[..., sl].astype(jnp.float32)).astype(h.dtype)
        outs.append(jnp.einsum('bsc,cd->bsd', u, w[gi]))
    return jnp.concatenate(outs, axis=-1) * scale


def conv_ffn(h, w_up, conv_w, conv_b, w_down):
    S = h.shape[1]
    u = h @ w_up
    up = jnp.pad(u, ((0, 0), (CONV_W - 1, 0), (0, 0)))
    u = sum(conv_w[j] * up[:, j:j + S] for j in range(CONV_W)) + conv_b
    a, g = u[..., :D_FF], u[..., D_FF:]
    return (jax.nn.silu(g) * a) @ w_down


def compress(blocks, pos_emb, w1, w2):
    z = blocks + pos_emb
    z = z.reshape(z.shape[:3] + (CMP_LEN * HEAD_DIM,))
    return jax.nn.gelu(z @ w1) @ w2


def nsa_shared_kv(x, positions, norm_kv, w_kv, cmp_pos_k, cmp_w1_k, cmp_w2_k, cmp_pos_v, cmp_w1_v, cmp_w2_v):
    B, S, _ = x.shape
    h = rmsnorm(x, norm_kv)
    kv = (h @ w_kv).reshape(B, S, 2 * N_BRANCH, N_KV_GROUPS, HEAD_DIM).transpose(2, 0, 3, 1, 4)
    k_c, v_c, k_s, v_s, k_w, v_w = [kv[i] for i in range(2 * N_BRANCH)]
    cos, sin = rope_tables(positions)
    k_s = rope(k_s, cos[:, None], sin[:, None])
    k_w = rope(k_w, cos[:, None], sin[:, None])
    n_chunk = S // CMP_STRIDE
    n_cmp = n_chunk - 1
    def blocks_of(z):
        ch = z.reshape(B, N_KV_GROUPS, n_chunk, CMP_STRIDE, HEAD_DIM)
        return jnp.concatenate([ch[:, :, :-1], ch[:, :, 1:]], axis=3)
    kc = compress(blocks_of(k_c), cmp_pos_k, cmp_w1_k, cmp_w2_k)
    vc = compress(blocks_of(v_c), cmp_pos_v, cmp_w1_v, cmp_w2_v)
    end_idx = CMP_STRIDE * jnp.arange(n_cmp) + CMP_LEN - 1
    cos_c, sin_c = rope_tables(positions[:, end_idx])
    kc = rope(kc, cos_c[:, None], sin_c[:, None])
    n_slc = S // SLC_BLOCK
    ks_blk = k_s.reshape(B, N_KV_GROUPS, n_slc, SLC_BLOCK, HEAD_DIM)
    vs_blk = v_s.reshape(B, N_KV_GROUPS, n_slc, SLC_BLOCK, HEAD_DIM)
    pad = ((0, 0), (0, 0), (WINDOW, 0), (0, 0))
    kw_pad, vw_pad = jnp.pad(k_w, pad), jnp.pad(v_w, pad)
    return (kc, vc, ks_blk, vs_blk, kw_pad, vw_pad)


def nsa_attention(h, w_in, w_out, cos, sin, kv):
    kc, vc, ks_blk, vs_blk, kw_pad, vw_pad = kv
    B, S, _ = h.shape
    n_cmp = kc.shape[2]
    n_slc = ks_blk.shape[2]
    n_top = min(N_SELECT, n_slc)
    scale = HEAD_DIM ** -0.5
    proj = h @ w_in
    q = proj[..., :N_HEADS * HEAD_DIM].reshape(B, S, N_HEADS, HEAD_DIM)
    q = rope(q, cos[:, :, None], sin[:, :, None])
    q = q.reshape(B, S, N_KV_GROUPS, HEADS_PER_GROUP, HEAD_DIM).transpose(0, 2, 3, 1, 4)
    gates = jax.nn.sigmoid(proj[..., N_HEADS * HEAD_DIM:].astype(jnp.float32))
    gates = gates.reshape(B, S, N_KV_GROUPS, HEADS_PER_GROUP, N_BRANCH).transpose(0, 2, 3, 1, 4)

    c_idx = jnp.arange(n_cmp)
    c_end = CMP_STRIDE * c_idx + CMP_LEN - 1
    blk = jnp.arange(n_slc)
    c_start = CMP_STRIDE * c_idx
    overlap = ((c_start[:, None] < (blk[None] + 1) * SLC_BLOCK)
               & (c_start[:, None] + CMP_LEN > blk[None] * SLC_BLOCK)).astype(jnp.float32)
    gather = jax.vmap(jax.vmap(lambda kb, ix: kb[ix]))

    def block_fn(s):
        t = s + jnp.arange(Q_BLOCK)
        qb = lax.dynamic_slice_in_dim(q, s, Q_BLOCK, axis=3)
        gb = lax.dynamic_slice_in_dim(gates, s, Q_BLOCK, axis=3)
        sc = jnp.einsum('bghqd,bgcd->bghqc', qb, kc).astype(jnp.float32) * scale
        p_c = masked_softmax(sc, c_end[None, :] <= t[:, None])
        o_c = jnp.einsum('bghqc,bgcd->bghqd', p_c.astype(vc.dtype), vc)
        imp = jnp.einsum('bghqc,cn->bgqn', p_c, overlap)
        cur = t // SLC_BLOCK
        forced = (blk[None] == 0) | (blk[None] == cur[:, None]) | (blk[None] == cur[:, None] - 1)
        valid = blk[None] * SLC_BLOCK <= t[:, None]
        score = jnp.where(valid, jnp.where(forced, FORCE_SCORE, imp), NEG)
        vals, idx = lax.top_k(score, n_top)
        ok = vals > 0.5 * NEG
        ks_g = gather(ks_blk, idx).reshape(B, N_KV_GROUPS, Q_BLOCK, n_top * SLC_BLOCK, HEAD_DIM)
        vs_g = gather(vs_blk, idx).reshape(B, N_KV_GROUPS, Q_BLOCK, n_top * SLC_BLOCK, HEAD_DIM)
        kpos = idx[..., None] * SLC_BLOCK + jnp.arange(SLC_BLOCK)
        mask_s = (ok[..., None] & (kpos <= t[:, None, None])).reshape(B, N_KV_GROUPS, Q_BLOCK, n_top * SLC_BLOCK)
        ss = jnp.einsum('bghqd,bgqkd->bghqk', qb, ks_g).astype(jnp.float32) * scale
        p_s = masked_softmax(ss, mask_s[:, :, None])
        o_s = jnp.einsum('bghqk,bgqkd->bghqd', p_s.astype(vs_g.dtype), vs_g)
        kw = lax.dynamic_slice_in_dim(kw_pad, s, WINDOW + Q_BLOCK, axis=2)
        vw = lax.dynamic_slice_in_dim(vw_pad, s, WINDOW + Q_BLOCK, axis=2)
        kpos_w = s - WINDOW + jnp.arange(WINDOW + Q_BLOCK)
        diff = t[:, None] - kpos_w[None]
        mask_w = (diff >= 0) & (diff < WINDOW) & (kpos_w[None] >= 0)
        sw = jnp.einsum('bghqd,bgkd->bghqk', qb, kw).astype(jnp.float32) * scale
        p_w = masked_softmax(sw, mask_w)
        o_w = jnp.einsum('bghqk,bgkd->bghqd', p_w.astype(vw.dtype), vw)
        out = (gb[..., 0:1] * o_c.astype(jnp.float32) + gb[..., 1:2] * o_s.astype(jnp.float32)
               + gb[..., 2:3] * o_w.astype(jnp.float32))
        return out.astype(h.dtype)

    starts = jnp.arange(S // Q_BLOCK) * Q_BLOCK
    o = lax.map(block_fn, starts)
    o = o.transpose(1, 0, 4, 2, 3, 5).reshape(B, S, N_HEADS * HEAD_DIM)
    return o @ w_out


def setup_inputs(seed: int = 0) -> dict:
    key = jax.random.key(seed)
    ks = iter(jax.random.split(key, 40))
    f32 = jnp.float32
    def w(shape, fan):
        return jax.random.normal(next(ks), shape, f32) * (fan ** -0.5)
    def gain(shape):
        return 1.0 + 0.05 * jax.random.normal(next(ks), shape, f32)
    def small(shape, s=0.02):
        return s * jax.random.normal(next(ks), shape, f32)
    q_cols = N_HEADS * HEAD_DIM + N_HEADS * N_BRANCH
    return {
        "x": jax.random.normal(next(ks), (BATCH, SEQ, D_MODEL), f32),
        "p": jax.random.normal(next(ks), (DEPTH, BATCH, SEQ, PLE_DIM), f32),
        "positions": jnp.broadcast_to(jnp.arange(SEQ, dtype=jnp.int32), (BATCH, SEQ)),
        "norm_mix": gain((DEPTH, D_MODEL)),
        "pool_w": w((N_A, len(POOL_WINDOWS), POOL_GROUP, POOL_GROUP), POOL_GROUP),
        "pool_scale": gain((N_A, D_MODEL)),
        "norm_kv": gain((D_MODEL,)),
        "w_kv": w((D_MODEL, 2 * N_BRANCH * N_KV_GROUPS * HEAD_DIM), D_MODEL),
        "cmp_pos_k": small((CMP_LEN, HEAD_DIM), 0.1),
        "cmp_w1_k": w((CMP_LEN * HEAD_DIM, CMP_HIDDEN), CMP_LEN * HEAD_DIM),
        "cmp_w2_k": w((CMP_HIDDEN, HEAD_DIM), CMP_HIDDEN),
        "cmp_pos_v": small((CMP_LEN, HEAD_DIM), 0.1),
        "cmp_w1_v": w((CMP_LEN * HEAD_DIM, CMP_HIDDEN), CMP_LEN * HEAD_DIM),
        "cmp_w2_v": w((CMP_HIDDEN, HEAD_DIM), CMP_HIDDEN),
        "w_in_b": w((N_B, D_MODEL, q_cols), D_MODEL),
        "w_out_b": w((N_B, N_HEADS * HEAD_DIM, D_MODEL), N_HEADS * HEAD_DIM),
        "norm_ffn": gain((DEPTH, D_MODEL)),
        "ffn_up": w((DEPTH, D_MODEL, 2 * D_FF), D_MODEL),
        "ffn_conv": w((DEPTH, CONV_W, 2 * D_FF), CONV_W),
        "ffn_conv_b": small((DEPTH, 2 * D_FF)),
        "ffn_down": w((DEPTH, D_FF, D_MODEL), D_FF),
        "norm_ple": gain((DEPTH, D_MODEL)),
        "ple_gate": w((DEPTH, D_MODEL, D_MODEL), D_MODEL),
        "ple_proj": w((DEPTH, PLE_DIM, D_MODEL), PLE_DIM),
        "norm_final": gain((D_MODEL,)),
    }


def reference(x, p, positions, norm_mix, pool_w, pool_scale, norm_kv, w_kv,
              cmp_pos_k, cmp_w1_k, cmp_w2_k, cmp_pos_v, cmp_w1_v, cmp_w2_v,
              w_in_b, w_out_b, norm_ffn, ffn_up, ffn_conv, ffn_conv_b, ffn_down,
              norm_ple, ple_gate, ple_proj, norm_final):
    cos, sin = rope_tables(positions)
    kv = None
    for i in range(DEPTH):
        if i < N_A:
            x = x + pool_mixer(rmsnorm(x, norm_mix[i]), pool_w[i], pool_scale[i])
        else:
            if i == N_A:
                kv = nsa_shared_kv(x, positions, norm_kv, w_kv, cmp_pos_k, cmp_w1_k, cmp_w2_k,
                                   cmp_pos_v, cmp_w1_v, cmp_w2_v)
            j = i - N_A
            x = x + nsa_attention(rmsnorm(x, norm_mix[i]), w_in_b[j], w_out_b[j], cos, sin, kv)
        x = x + conv_ffn(rmsnorm(x, norm_ffn[i]), ffn_up[i], ffn_conv[i], ffn_conv_b[i], ffn_down[i])
        gate = jax.nn.sigmoid(rmsnorm(x, norm_ple[i]) @ ple_gate[i])
        x = x + gate * (p[i] @ ple_proj[i])
    return rmsnorm(x, norm_final)
```

```python
import numpy as np
import os
from contextlib import ExitStack
import concourse.bass as bass
import concourse.mybir as mybir
from concourse.bass_utils import run_bass_kernel_spmd

F32 = mybir.dt.float32
BF16 = mybir.dt.bfloat16
I32 = mybir.dt.int32
ALU = mybir.AluOpType
AF = mybir.ActivationFunctionType
AX = mybir.AxisListType

ENGS = ["pe", "act", "dve", "pool", "sp"]

D = 1024
KC = 8
B = 4
SEQ = 4096
OWN = 2048
HALO = 64
T = HALO + OWN
PAD = 16
XC = PAD + T
DFF = 2816
NFC = 22
PLE = 256
EPS = 1e-6
NCORES = 8
WINS = (2, 4, 8, 16)

V_MIX, V_FFN, V_PLE, V_PSC, V_KV, V_FIN = 0, 4, 8, 12, 14, 15
NVEC = 16


class Buf:
    __slots__ = ("name", "w", "r")

    def __init__(self, name=""):
        self.name = name
        self.w = None
        self.r = {}


class Sched:
    def __init__(self, nc):
        self.nc = nc
        self.ops = {e: [] for e in ENGS}
        self.cnt = {e: 0 for e in ENGS}
        self.dcnt = {}
        self.seen = {e: {} for e in ENGS}
        self.stack = ExitStack()
        self.nops = 0
        self.dma_bufs = []

    def sbuf(self, name, shape, dt):
        return self.stack.enter_context(self.nc.sbuf_tensor("sb_" + name, list(shape), dt))

    def psum(self, name, shape, dt=F32):
        return self.stack.enter_context(self.nc.psum_tensor("ps_" + name, list(shape), dt))

    def _waits(self, eng, reads, writes):
        need = {}
        for b in reads:
            if b.w is not None:
                k, v = b.w
                if v > need.get(k, 0):
                    need[k] = v
        for b in writes:
            if b.w is not None:
                k, v = b.w
                if v > need.get(k, 0):
                    need[k] = v
            for k, v in b.r.items():
                if v > need.get(k, 0):
                    need[k] = v
        waits = []
        seen = self.seen[eng]
        for k, v in need.items():
            if eng == "pe" and k == "pe":
                continue
            if k in self.cnt:
                assert v <= self.cnt[k], f"unmarked producer on {k} needed by {eng}"
            if seen.get(k, 0) >= v:
                continue
            seen[k] = v
            waits.append((k, v))
        return waits

    def _tag(self, tag, reads, writes):
        k, v = tag
        for b in reads:
            if b.r.get(k, 0) < v:
                b.r[k] = v
        for b in writes:
            b.w = tag
            b.r = {}

    def op(self, eng, fn, reads=(), writes=(), mark=True):
        waits = self._waits(eng, reads, writes)
        if mark:
            self.cnt[eng] += 1
            tag = (eng, self.cnt[eng])
            inc = (eng, 1)
        else:
            tag = (eng, self.cnt[eng] + 1)
            inc = None
        self._tag(tag, reads, writes)
        self.ops[eng].append((waits, fn, inc))
        self.nops += 1

    def dma(self, q, out, in_, reads=(), writes=(), sem="d0", **kw):
        waits = self._waits(q, reads, writes)
        key = "dma:" + sem
        self.dcnt[key] = self.dcnt.get(key, 0) + 16
        tag = (key, self.dcnt[key])
        self._tag(tag, reads, writes)
        for b in writes:
            if b not in self.dma_bufs:
                self.dma_bufs.append(b)
        self.ops[q].append((waits, lambda e: e.dma_start(out=out, in_=in_, **kw), (key, 16)))
        self.nops += 1

    def barrier(self):
        for e in ("pe", "act", "dve", "pool"):
            waits = []
            for k, v in list(self.cnt.items()) + list(self.dcnt.items()):
                if k == e or v == 0:
                    continue
                if self.seen[e].get(k, 0) >= v:
                    continue
                self.seen[e][k] = v
                waits.append((k, v))
            self.ops[e].append((waits, None, None))

    def wait_all(self, eng, bufs):
        waits = self._waits(eng, bufs, ())
        self.ops[eng].append((waits, None, None))

    def emit(self):
        nc = self.nc
        self.wait_all("sp", [b for b in self.dma_bufs if b.w is not None and b.w[0].startswith("dma:")])
        sems = {}
        for k in list(self.cnt.keys()) + list(self.dcnt.keys()):
            sems[k] = self.stack.enter_context(nc.semaphore("s_" + k.replace(":", "_")))
        engmap = {"pe": "tensor", "act": "scalar", "dve": "vector", "pool": "gpsimd", "sp": "sync"}
        with nc.Block() as block:
            for e in ENGS:
                ops = self.ops[e]
                if not ops:
                    continue

                def body(engine, ops=ops):
                    for waits, fn, inc in ops:
                        for k, v in waits:
                            engine.wait_ge(sems[k], v)
                        if fn is not None:
                            ins = fn(engine)
                            if inc is not None:
                                ins.then_inc(sems[inc[0]], inc[1])
                getattr(block, engmap[e])(body)

    def close(self):
        self.stack.close()


class Ring:
    def __init__(self, S, name, n, shape, dt, psum=False):
        self.slots = []
        for i in range(n):
            t = S.psum(f"{name}{i}", shape, dt) if psum else S.sbuf(f"{name}{i}", shape, dt)
            self.slots.append((t, Buf(f"{name}{i}")))
        self.i = 0

    def next(self):
        s = self.slots[self.i % len(self.slots)]
        self.i += 1
        return s

    def nexts(self):
        k = self.i % len(self.slots)
        t, b = self.next()
        return t, b, f"{b.name}_{k}"


class Prog:
    def __init__(self, phase):
        self.phase = phase
        nc = bass.Bass("TRN2", target_bir_lowering=False)
        self.nc = nc
        self.S = Sched(nc)
        self.dram = {}
        self.dsem = 0

    def din(self, name, shape, dt=F32):
        self.dram[name] = self.nc.dram_tensor(name, list(shape), dt, kind="ExternalInput").ap()
        return self.dram[name]

    def dout(self, name, shape, dt=F32):
        self.dram[name] = self.nc.dram_tensor(name, list(shape), dt, kind="ExternalOutput").ap()
        return self.dram[name]

    def newsem(self, base):
        self.dsem += 1
        return f"{base}{self.dsem}"

    def setup_common(self, stream=True):
        S = self.S
        self.pb = [(S.psum(f"pb{i}", [128, 512], F32), Buf(f"pb{i}")) for i in range(7)]
        self.pbt = (S.psum("pbt", [128, 1024], BF16), Buf("pbt"))
        if stream:
            self.xT = S.sbuf("xT", [128, KC, XC], F32)
            self.xbufs = [Buf(f"x{i}") for i in range((XC + 63) // 64)]
        self.epsb = S.sbuf("epsb", [128, 1], F32); self.bepsb = Buf("epsb")
        S.op("dve", lambda e: e.memset(self.epsb[:], EPS), writes=[self.bepsb])
        self.vecs = S.sbuf("vecs", [128, NVEC, KC], F32)
        self.bvecs = Buf("vecs")
        S.dma("sp", self.vecs[:], self.din("vecs", [128, NVEC, KC])[:, :, :], writes=[self.bvecs], sem="c_vecs")
        self.ones = S.sbuf("ones", [128, 128], F32)
        self.bones = Buf("ones")
        S.op("dve", lambda e: e.memset(self.ones[:], 1.0), writes=[self.bones])
        self.onesb16 = S.sbuf("onesb16", [128, 128], BF16)
        S.op("dve", lambda e: e.memset(self.onesb16[:], 1.0), writes=[self.bones])
        self.convp = S.sbuf("convp", [128, 4, 2 * NFC, 4], F32)
        self.bconvp = Buf("convp")
        S.dma("sp", self.convp[:], self.din("convp", [128, 4, 2 * NFC, 4])[:, :, :, :], writes=[self.bconvp], sem="c_convp")

    def xB(self, c0, c1):
        return self.xbufs[c0 // 64:(c1 + 63) // 64]

    def norm_stats(self, c0, n, src=None, sb=None):
        S = self.S
        xT = self.xT if src is None else src
        xrd = self.xB(c0, c0 + n) if sb is None else sb
        ps, bps = self.pb[6]
        for k in range(KC):
            sq, bsq = self.sq_ring.next()
            S.op("act", lambda e, k=k, sq=sq: e.activation(out=sq[:, 0:n], in_=xT[:, k, c0:c0 + n], func=AF.Square),
                 reads=xrd, writes=[bsq])
            ones_ = self.onesb16 if getattr(self, "norm_bf16", False) else self.ones
            S.op("pe", lambda e, k=k, sq=sq, ones_=ones_: e.matmul(ps[:, 0:n], lhsT=ones_[:], rhs=sq[:, 0:n], start=(k == 0), stop=(k == KC - 1)),
                 reads=[bsq, self.bones], writes=[bps], mark=True)
        rs, brs = self.rs_ring.next()
        S.op("act", lambda e: e.activation(out=rs[:, 0:n], in_=ps[:, 0:n], func=AF.Sqrt, scale=1.0 / D, bias=self.epsb[:, 0:1]),
             reads=[bps, self.bepsb], writes=[brs])
        S.op("dve", lambda e: e.reciprocal(out=rs[:, 0:n], in_=rs[:, 0:n]), reads=[brs], writes=[brs])
        return rs, brs

    def norm_apply(self, out, bout, o0, c0, n, vidx, rs, brs, eng="dve", ks=None, okoff=0, src=None, sb=None):
        S = self.S
        xT = self.xT if src is None else src
        xrd = self.xB(c0, c0 + n) if sb is None else sb
        for k in (range(KC) if ks is None else ks):
            S.op(eng, lambda e, k=k: e.scalar_tensor_tensor(out=out[:, k - okoff, o0:o0 + n], in0=xT[:, k, c0:c0 + n],
                                                            scalar=self.vecs[:, vidx, k:k + 1], in1=rs[:, 0:n],
                                                            op0=ALU.mult, op1=ALU.mult),
                 reads=xrd + [self.bvecs, brs], writes=[bout])

    FC_HALVES = (tuple(range(0, 12)), tuple(range(12, 22)))

    def ffn(self, layer, chunks, groups):
        S = self.S
        wl = getattr(self, "wl", layer)
        upv = self.dram["ffn_up"][wl].rearrange("(kc p) n -> p kc n", p=128)
        dnv = self.dram["ffn_down"][wl].rearrange("(fc p) d -> p fc d", p=128)
        cp = self.convp
        for grp in groups:
            hts = []
            offs = []
            o = 0
            for gi_, ci in enumerate(grp):
                c0, n = chunks[ci]
                rs, brs = self.norm_stats(c0 - 2, n + 2)
                ht, bht = self.ht_ring.next()
                self.norm_apply(ht, bht, 0, c0 - 2, n + 2, V_FFN + layer, rs, brs)
                if gi_ == 0 and grp is not groups[0]:
                    S.op("pool", lambda e, ht=ht: e.tensor_copy(out=ht[:, :, 0:2], in_=self.hcarry[:, :, :]),
                         reads=[self.bhcarry, bht], writes=[bht])
                if gi_ == len(grp) - 1:
                    S.op("pool", lambda e, ht=ht, n=n: e.tensor_copy(out=self.hcarry[:, :, :], in_=ht[:, :, n:n + 2]),
                         reads=[bht], writes=[self.bhcarry])
                hts.append((ht, bht))
                offs.append(o)
                o += n
            for fcs in self.FC_HALVES:
                nf = len(fcs)
                for fb in range(nf // 2):
                    wt, bwt = self.wup_ring.next()
                    slot = (self.wup_ring.i - 1) % len(self.wup_ring.slots)
                    for ag in range(2):
                        col0 = ag * DFF + (fcs[0] + fb * 2) * 128
                        S.dma("pool", wt[:, :, ag, :], upv[:, :, col0:col0 + 256], writes=[bwt], sem=f"wup{slot}")
                    for f2 in range(2):
                        fl = fb * 2 + f2
                        fc = fcs[0] + fl
                        for gi, ci in enumerate(grp):
                            c0, n = chunks[ci]
                            ht, bht = hts[gi]
                            tt = []
                            for ag in range(2):
                                ps, bps = self.up_ring.next()
                                for k in range(KC):
                                    S.op("pe", lambda e, k=k, ps=ps, ag=ag, f2=f2, ht=ht, wt=wt, n=n: e.matmul(
                                        ps[:, 0:n + 2], lhsT=wt[:, k, ag, f2 * 128:(f2 + 1) * 128], rhs=ht[:, k, 0:n + 2],
                                        start=(k == 0), stop=(k == KC - 1)),
                                        reads=[bwt, bht], writes=[bps], mark=(k == KC - 1))
                                tmp, btmp = self.cv_ring.next()
                                j = ag * NFC + fc
                                S.op("act", lambda e, ps=ps, tmp=tmp, j=j, n=n: e.activation(
                                    out=tmp[:, 0:n], in_=ps[:, 2:n + 2], func=AF.Identity,
                                    scale=cp[:, layer, j, 2:3], bias=cp[:, layer, j, 3:4]),
                                    reads=[bps, self.bconvp], writes=[btmp])
                                S.op("dve", lambda e, ps=ps, tmp=tmp, j=j, n=n: e.scalar_tensor_tensor(
                                    out=tmp[:, 0:n], in0=ps[:, 1:n + 1], scalar=cp[:, layer, j, 1:2], in1=tmp[:, 0:n],
                                    op0=ALU.mult, op1=ALU.add), reads=[bps, btmp, self.bconvp], writes=[btmp])
                                S.op("dve", lambda e, ps=ps, tmp=tmp, j=j, n=n: e.scalar_tensor_tensor(
                                    out=tmp[:, 0:n], in0=ps[:, 0:n], scalar=cp[:, layer, j, 0:1], in1=tmp[:, 0:n],
                                    op0=ALU.mult, op1=ALU.add), reads=[bps, btmp, self.bconvp], writes=[btmp])
                                tt.append((tmp, btmp))
                            (ta, bta), (tg, btg) = tt
                            S.op("act", lambda e, tg=tg, n=n: e.activation(out=tg[:, 0:n], in_=tg[:, 0:n], func=AF.Silu),
                                 reads=[btg], writes=[btg])
                            o = offs[gi]
                            S.op("pool", lambda e, ta=ta, tg=tg, fl=fl, o=o, n=n: e.tensor_tensor(
                                out=self.mT[:, fl, o:o + n], in0=ta[:, 0:n], in1=tg[:, 0:n], op=ALU.mult),
                                reads=[bta, btg], writes=[self.mB[fl][gi]])
                for db in range(KC // 2):
                    wd, bwd = self.wdn_ring.next()
                    slot = (self.wdn_ring.i - 1) % len(self.wdn_ring.slots)
                    S.dma("pool", wd[:, 0:nf, :], dnv[:, fcs[0]:fcs[0] + nf, db * 256:(db + 1) * 256], writes=[bwd], sem=f"wdn{slot}")
                    for d2 in range(2):
                        d = db * 2 + d2
                        for gi, ci in enumerate(grp):
                            c0, n = chunks[ci]
                            o = offs[gi]
                            ps, bps = self.dn_ring.next()
                            for fl in range(nf):
                                S.op("pe", lambda e, fl=fl, ps=ps, wd=wd, d2=d2, o=o, n=n: e.matmul(
                                    ps[:, 0:n], lhsT=wd[:, fl, d2 * 128:(d2 + 1) * 128], rhs=self.mT[:, fl, o:o + n],
                                    start=(fl == 0), stop=(fl == nf - 1)),
                                    reads=[bwd, self.mB[fl][gi]], writes=[bps], mark=(fl == nf - 1))
                            S.op("dve", lambda e, ps=ps, d=d, c0=c0, n=n: e.tensor_tensor(
                                out=self.xT[:, d, c0:c0 + n], in0=ps[:, 0:n], in1=self.xT[:, d, c0:c0 + n], op=ALU.add),
                                reads=[bps] + self.xB(c0, c0 + n), writes=self.xB(c0, c0 + n))

    def ple(self, layer, chunks, pT_ap):
        S = self.S
        wl = getattr(self, "wl", layer)
        wgv = self.dram["ple_gate"][wl].rearrange("(kc p) n -> p kc n", p=128)
        wpv = self.dram["ple_proj"][wl].rearrange("(kc p) n -> p kc n", p=128)
        S.dma("pool", self.wpp[:, :, :], wpv, writes=[self.bwpp], sem="wpp")
        for (c0, n, p0) in chunks:
            pt, bpt = self.pt_ring.next()
            slot = (self.pt_ring.i - 1) % len(self.pt_ring.slots)
            S.dma("pool", pt[:, :, 0:n], pT_ap[:, :, p0:p0 + n], writes=[bpt], sem=f"pt{slot}")
            rs, brs = self.norm_stats(c0, n)
            ht, bht = self.ht_ring.next()
            self.norm_apply(ht, bht, 0, c0, n, V_PLE + layer, rs, brs)
            for d in range(KC):
                wg, bwg = self.wg_ring.next()
                slot = (self.wg_ring.i - 1) % len(self.wg_ring.slots)
                S.dma("pool", wg[:, :, :], wgv[:, :, d * 128:(d + 1) * 128], writes=[bwg], sem=f"wg{slot}")
                pg, bpg = self.up_ring.next()
                for k in range(KC):
                    S.op("pe", lambda e, k=k, pg=pg, wg=wg, ht=ht, n=n: e.matmul(
                        pg[:, 0:n], lhsT=wg[:, k, :], rhs=ht[:, k, 0:n],
                        start=(k == 0), stop=(k == KC - 1)), reads=[bwg, bht], writes=[bpg], mark=(k == KC - 1))
                pp, bpp = self.up_ring.next()
                for k in range(2):
                    S.op("pe", lambda e, k=k, pp=pp, d=d, pt=pt, n=n: e.matmul(
                        pp[:, 0:n], lhsT=self.wpp[:, k, d * 128:(d + 1) * 128], rhs=pt[:, k, 0:n],
                        start=(k == 0), stop=(k == 1)), reads=[self.bwpp, bpt], writes=[bpp], mark=(k == 1))
                tmp, btmp = self.cv_ring.next()
                S.op("act", lambda e, pg=pg, tmp=tmp, n=n: e.activation(out=tmp[:, 0:n], in_=pg[:, 0:n], func=AF.Sigmoid),
                     reads=[bpg], writes=[btmp])
                S.op("dve", lambda e, pp=pp, tmp=tmp, n=n: e.tensor_tensor(out=tmp[:, 0:n], in0=pp[:, 0:n], in1=tmp[:, 0:n], op=ALU.mult),
                     reads=[bpp, btmp], writes=[btmp])
                S.op("dve", lambda e, tmp=tmp, d=d, c0=c0, n=n: e.tensor_tensor(
                    out=self.xT[:, d, c0:c0 + n], in0=tmp[:, 0:n], in1=self.xT[:, d, c0:c0 + n], op=ALU.add),
                    reads=[btmp] + self.xB(c0, c0 + n), writes=self.xB(c0, c0 + n))

    def pool_mixer(self, layer, chunks):
        S = self.S
        pw = self.dram["pool_w"][layer].rearrange("g (cc p) d -> p g cc d", p=128)
        for gi in range(4):
            S.dma("pool", self.wpool[:, gi, :, :], pw[:, gi, :, :], writes=[self.bwpool], sem="wpool")
        cy, bcy = self.carry, self.bcarry
        S.op("dve", lambda e: e.memset(cy[:, :, :], 0.0), writes=bcy)
        for (c0, n) in chunks:
            rs, brs = self.norm_stats(c0, n)
            ut, but = self.ht_ring.next()
            W = 16 + n
            for gi in range(4):
                g0 = 2 * gi
                hw, bhw = self.hw_ring.next()
                S.op("pool", lambda e, hw=hw, g0=g0: e.tensor_copy(out=hw[:, :, 0:16], in_=cy[:, g0:g0 + 2, :]),
                     reads=[bcy[gi]], writes=[bhw])
                self.norm_apply(hw, bhw, 16, c0, n, V_MIX + layer, rs, brs, ks=(g0, g0 + 1), okoff=g0)
                S.op("pool", lambda e, hw=hw, g0=g0, n=n: e.tensor_copy(out=cy[:, g0:g0 + 2, :], in_=hw[:, :, n:n + 16]),
                     reads=[bhw], writes=[bcy[gi]])
                src, bsrc = hw, bhw
                sh = 1
                lo = 0
                for lev in range(gi + 1):
                    dst, bdst = self.sa[lev % 2]
                    lo2 = lo + sh
                    eng = "dve" if (lev % 2 == 0) else "pool"
                    S.op(eng, lambda e, dst=dst, src=src, lo2=lo2, sh=sh, W=W: e.tensor_tensor(
                        out=dst[:, :, lo2:W], in0=src[:, :, lo2:W], in1=src[:, :, lo2 - sh:W - sh], op=ALU.add),
                        reads=[bsrc], writes=[bdst])
                    src, bsrc = dst, bdst
                    lo = lo2
                    sh *= 2
                win = WINS[gi]
                S.op("dve", lambda e, src=src, hw=hw, g0=g0, ut=ut, n=n, win=win: e.scalar_tensor_tensor(
                    out=ut[:, g0:g0 + 2, 0:n], in0=src[:, :, 16:16 + n], scalar=1.0 / win, in1=hw[:, :, 16:16 + n],
                    op0=ALU.mult, op1=ALU.subtract), reads=[bsrc, bhw], writes=[but])
                f0 = PAD + HALO
                if c0 <= f0 < c0 + n:
                    q = f0 - c0
                    fx, bfx = self.fix, self.bfix
                    S.op("dve", lambda e, src=src, q=q, gi=gi: e.tensor_tensor(
                        out=fx[:, :, :], in0=src[:, :, 16 + q:32 + q], in1=self.invcnt[:, gi, :, :], op=ALU.mult),
                        reads=[bsrc, self.binvcnt], writes=[bfx])
                    S.op("dve", lambda e, q=q, g0=g0, ut=ut, hw=hw: e.tensor_tensor(
                        out=ut[:, g0:g0 + 2, q:q + 16], in0=fx[:, :, :], in1=hw[:, :, 16 + q:32 + q], op=ALU.subtract),
                        reads=[bfx, bhw], writes=[but])
                for dd in range(2):
                    ps, bps = self.up_ring.next()
                    for cc in range(2):
                        S.op("pe", lambda e, ps=ps, gi=gi, cc=cc, dd=dd, ut=ut, g0=g0, n=n: e.matmul(
                            ps[:, 0:n], lhsT=self.wpool[:, gi, cc, dd * 128:(dd + 1) * 128], rhs=ut[:, g0 + cc, 0:n],
                            start=(cc == 0), stop=(cc == 1)), reads=[self.bwpool, but], writes=[bps], mark=(cc == 1))
                    d = g0 + dd
                    S.op("dve", lambda e, ps=ps, d=d, c0=c0, n=n: e.scalar_tensor_tensor(
                        out=self.xT[:, d, c0:c0 + n], in0=ps[:, 0:n], scalar=self.vecs[:, V_PSC + layer, d:d + 1],
                        in1=self.xT[:, d, c0:c0 + n], op0=ALU.mult, op1=ALU.add),
                        reads=[bps, self.bvecs] + self.xB(c0, c0 + n), writes=self.xB(c0, c0 + n))

    def alloc_work(self, gmax):
        S = self.S
        self.sq_ring = Ring(S, "sq", 2, [128, 512], F32)
        self.rs_ring = Ring(S, "rs", 2, [128, 512], F32)
        self.ht_ring = Ring(S, "ht", 3, [128, KC, 512], BF16)
        self.wup_ring = Ring(S, "wup", 2, [128, KC, 2, 256], BF16)
        self.wdn_ring = Ring(S, "wdn", 2, [128, 12, 256], BF16)
        self.wg_ring = Ring(S, "wgr", 2, [128, KC, 128], BF16)
        self.cv_ring = Ring(S, "cv", 4, [128, 512], F32)
        self.pt_ring = Ring(S, "pt", 2, [128, 2, 512], BF16)
        self.up_ring = Ring.__new__(Ring); self.up_ring.slots = self.pb[0:4]; self.up_ring.i = 0
        self.dn_ring = Ring.__new__(Ring); self.dn_ring.slots = self.pb[4:6]; self.dn_ring.i = 0
        self.mT = S.sbuf("mT", [128, 12, gmax], BF16)
        self.mB = [[Buf(f"m{fc}_{g}") for g in range(3)] for fc in range(12)]
        self.wpp = S.sbuf("wpp", [128, 2, D], BF16); self.bwpp = Buf("wpp")
        self.hcarry = S.sbuf("hcarry", [128, KC, 2], BF16); self.bhcarry = Buf("hcarry")

    def build_A(self):
        S = self.S
        self.setup_common()
        for nm, shp in (("ffn_up", [2, D, 2 * DFF]), ("ffn_down", [2, DFF, D]), ("ple_gate", [2, D, D]),
                        ("ple_proj", [2, PLE, D]), ("pool_w", [2, 4, 256, 256])):
            self.din(nm, shp)
        xin = self.din("xT0", [128, KC, XC])
        pTin = self.din("pT", [2, 128, 2, T])
        flag = self.din("flag", [128, 1])
        invc = self.din("invcnt", [128, 4, 2, 16])
        self.alloc_work(1092)
        self.wpool = S.sbuf("wpool", [128, 4, 2, 256], BF16); self.bwpool = Buf("wpool")
        self.hw_ring = Ring(S, "hw", 2, [128, 2, 528], F32)
        self.carry = S.sbuf("carry", [128, KC, 16], F32); self.bcarry = [Buf(f"cy{i}") for i in range(4)]
        self.sa = [(S.sbuf(f"sa{i}", [128, 2, 528], F32), Buf(f"sa{i}")) for i in range(2)]
        self.fix = S.sbuf("fix", [128, 2, 16], F32); self.bfix = Buf("fix")
        self.invcnt = S.sbuf("invcnt", [128, 4, 2, 16], F32); self.binvcnt = Buf("invcnt")
        S.dma("sp", self.invcnt[:], invc[:, :, :, :], writes=[self.binvcnt], sem="c_inv")
        self.flag = S.sbuf("flag", [128, 1], F32); self.bflag = Buf("flag")
        S.dma("sp", self.flag[:], flag[:, :], writes=[self.bflag], sem="c_flag")
        for k in range(KC):
            S.dma("sp", self.xT[:, k, :], xin[:, k, :], writes=self.xbufs, sem="xin")
        mix_chunks = [(PAD + 512 * i, 512) for i in range(4)] + [(PAD + 2048, 64)]
        ffn_chunks = [(PAD + 510 * i, 510) for i in range(4)] + [(PAD + 2040, 72)]
        ffn_groups = [[0, 1], [2, 3, 4]]
        for layer in range(2):
            self.pool_mixer(layer, mix_chunks)
            self.ffn(layer, ffn_chunks, ffn_groups)
            self.ple(layer, [(c0, n, c0 - PAD) for (c0, n) in mix_chunks], pTin[layer])
            if layer == 0:
                S.op("dve", lambda e: e.tensor_scalar(out=self.xT[:, :, PAD:PAD + HALO], in0=self.xT[:, :, PAD:PAD + HALO],
                                                      scalar1=self.flag[:, 0:1], scalar2=None, op0=ALU.mult),
                     reads=self.xB(PAD, PAD + HALO) + [self.bflag], writes=self.xB(PAD, PAD + HALO))
        xo = self.dout("x2T", [128, KC, OWN])
        bo = Buf("x2T")
        for k in range(KC):
            S.dma("sp", xo[:, k, :], self.xT[:, k, PAD + HALO:XC], reads=self.xB(PAD + HALO, XC), writes=[bo], sem="xout")
        S.wait_all("sp", [bo])
        S.emit()
        S.close()
        return self.nc


NG = 4
HD = 64
LOC = 4096
WK0 = 1536
MAGIC = 12582912.0
C1_2PI = 6.28125
C2_2PI = float(2 * np.pi - 6.28125)


def rope_tables(P, posf, bposf, n, CT, ST, bT):
    S = P.S
    ang, bang = P.rt_a, P.brt_a
    kk, bkk = P.rt_k, P.brt_k
    S.op("dve", lambda e: e.tensor_scalar(out=ang[:, 0:n], in0=posf[:, 0:n], scalar1=P.ropec[:, 0:1], scalar2=None, op0=ALU.mult),
         reads=[bposf, P.bropec], writes=[bang])
    for which, dst in ((0, ST), (1, CT)):
        if which == 1:
            S.op("dve", lambda e: e.tensor_scalar(out=ang[:, 0:n], in0=ang[:, 0:n], scalar1=float(np.pi / 2), scalar2=None, op0=ALU.add),
                 reads=[bang], writes=[bang])
        S.op("dve", lambda e: e.tensor_scalar(out=kk[:, 0:n], in0=ang[:, 0:n], scalar1=float(1 / (2 * np.pi)), scalar2=MAGIC, op0=ALU.mult, op1=ALU.add),
             reads=[bang], writes=[bkk])
        S.op("dve", lambda e: e.tensor_scalar(out=kk[:, 0:n], in0=kk[:, 0:n], scalar1=MAGIC, scalar2=None, op0=ALU.subtract),
             reads=[bkk], writes=[bkk])
        S.op("dve", lambda e, dst=dst: e.scalar_tensor_tensor(out=dst[:, 0:n], in0=kk[:, 0:n], scalar=-C1_2PI, in1=ang[:, 0:n], op0=ALU.mult, op1=ALU.add),
             reads=[bkk, bang], writes=[bT])
        S.op("dve", lambda e, dst=dst: e.scalar_tensor_tensor(out=dst[:, 0:n], in0=kk[:, 0:n], scalar=-C2_2PI, in1=dst[:, 0:n], op0=ALU.mult, op1=ALU.add),
             reads=[bkk, bT], writes=[bT])
        S.op("act", lambda e, dst=dst: e.activation(out=dst[:, 0:n], in_=dst[:, 0:n], func=AF.Sin), reads=[bT], writes=[bT])
    S.op("dve", lambda e: e.tensor_scalar(out=ST[:, 0:n], in0=ST[:, 0:n], scalar1=P.ropec[:, 1:2], scalar2=None, op0=ALU.mult),
         reads=[bT, P.bropec], writes=[bT])


def rope_alloc(P):
    S = P.S
    P.rt_a = S.sbuf("rt_a", [16, 512], F32); P.brt_a = Buf()
    P.rt_k = S.sbuf("rt_k", [16, 512], F32); P.brt_k = Buf()
    P.ropec = S.sbuf("ropec", [16, 2], F32); P.bropec = Buf()
    S.dma("sp", P.ropec[:], P.din("ropec", [16, 2])[:, :], writes=[P.bropec], sem="c_ropec")


def rope_evac(P, dst, bdst, n, pa, bpa, pr, bpr, CT, ST, bT, c_sl):
    S = P.S
    RS = float(os.environ.get("RE_STOP", "99"))
    if RS < 1:
        return
    S.op("act", lambda e: e.activation(out=dst[0:64, 0:n], in_=pa[0:64, 0:n], func=AF.Identity), reads=[bpa], writes=[bdst])
    if RS < 2:
        return
    t1, bt1 = P.rp_ring.next()
    t2, bt2 = P.rp_ring.next()
    S.op("dve", lambda e: e.tensor_tensor(out=t1[:, 0:n], in0=pa[0:16, 0:n], in1=CT[:, c_sl], op=ALU.mult), reads=[bpa, bT, bdst], writes=[bt1])
    if RS < 3:
        return
    S.op("dve", lambda e: e.tensor_tensor(out=t2[:, 0:n], in0=pr[0:16, 0:n], in1=ST[:, c_sl], op=ALU.mult), reads=[bpr, bT], writes=[bt2])
    if RS < 4:
        return
    S.op("dve", lambda e: e.tensor_tensor(out=dst[0:16, 0:n], in0=t1[:, 0:n], in1=t2[:, 0:n], op=ALU.add), reads=[bt1, bt2, bdst], writes=[bdst])


def build_KV(P):
    import os
    SKIP = set(os.environ.get('KV_SKIP', '').split(','))
    STOP = float(os.environ.get('KV_STOP', '99'))
    class _Stop(Exception):
        pass
    def stage(n):
        if n > STOP:
            raise _Stop()
    S = P.S
    P.setup_common(stream=False)
    xloc = P.din("xloc", [128, KC, LOC])
    pos16 = P.din("pos16", [16, LOC], I32)
    kvalid_in = P.din("keyvalid", [128, LOC // 128])
    wkv_in = P.din("w_kv", [D, 6 * NG * HD]).rearrange("(kc p) n -> p kc n", p=128)
    w1_in = [P.din(f"cmp_w1_{t}", [2048, 128]).rearrange("(j d) h -> d j h", d=64) for t in "kv"]
    w2_in = [P.din(f"cmp_w2_{t}", [128, 64]) for t in "kv"]
    posT_in = [P.din(f"cmp_posT_{t}", [64, 32]) for t in "kv"]
    ident_in = P.din("identb", [128, 128], BF16)
    o_KS = P.dout("KS", [NG, 64, LOC], BF16)
    o_KW = P.dout("KW", [NG, 64, LOC - WK0], BF16)
    o_VS = P.dout("VS", [NG, 128, LOC // 128, 128], BF16)
    o_VW = P.dout("VW", [NG, 128, (LOC - WK0) // 128, 128], BF16)
    o_KC = P.dout("KC", [NG, 64, 256], BF16)
    o_VC = P.dout("VC", [NG, 128, 2, 128], BF16)
    bout = Buf("kvout")
    rope_alloc(P)
    try:
      return _kv_body(P, S, SKIP, stage, locals())
    except _Stop:
      pass
    S.wait_all('sp', [bout])
    S.emit()
    S.close()
    return P.nc


def _kv_body(P, S, SKIP, stage, L):
    xloc, pos16, kvalid_in, wkv_in, w1_in, w2_in, posT_in, ident_in = (L[k] for k in ('xloc', 'pos16', 'kvalid_in', 'wkv_in', 'w1_in', 'w2_in', 'posT_in', 'ident_in'))
    o_KS, o_KW, o_VS, o_VW, o_KC, o_VC, bout = (L[k] for k in ('o_KS', 'o_KW', 'o_VS', 'o_VW', 'o_KC', 'o_VC', 'bout'))
    wkv = S.sbuf("wkv", [128, KC, 6 * NG * HD], BF16); bwkv = Buf()
    for k in range(KC):
        S.dma("pool", wkv[:, k, :], wkv_in[:, k, :], writes=[bwkv], sem="wkv")
    stage(1)
    wrot = S.sbuf("wrot", [128, KC, 2, NG, 32], BF16); bwrot = Buf()
    S.op("pool", lambda e: e.memset(wrot[:], 0.0), writes=[bwrot])
    for ti, ty in enumerate((2, 4)):
        v = wkv[:, :, ty * 256:(ty + 1) * 256].rearrange("p k (g d) -> p k g d", g=NG)
        S.op("pool", lambda e, ti=ti, v=v: e.tensor_copy(out=wrot[:, :, ti, :, 0:8], in_=v[:, :, :, 8:16]), reads=[bwkv], writes=[bwrot])
        S.op("pool", lambda e, ti=ti, v=v: e.tensor_copy(out=wrot[:, :, ti, :, 8:16], in_=v[:, :, :, 0:8]), reads=[bwkv], writes=[bwrot])
    stage(2)
    w1 = []; w2 = []; posT = []
    for t in range(2):
        a = S.sbuf(f"w1_{t}", [64, 32, 128], BF16); ba = Buf()
        S.dma("pool", a[:, :, :], w1_in[t], writes=[ba], sem=f"w1_{t}")
        w1.append((a, ba))
        b2 = S.sbuf(f"w2_{t}", [128, 64], BF16); bb2 = Buf()
        S.dma("pool", b2[:, :], w2_in[t][:, :], writes=[bb2], sem=f"w2_{t}")
        w2.append((b2, bb2))
        c = S.sbuf(f"posT_{t}", [64, 32], BF16); bc = Buf()
        S.dma("pool", c[:, :], posT_in[t][:, :], writes=[bc], sem=f"posT_{t}")
        posT.append((c, bc))
    stage(3)
    w2rot = S.sbuf("w2rot", [128, 32], BF16); bw2rot = Buf()
    S.op("pool", lambda e: e.memset(w2rot[:], 0.0), writes=[bw2rot])
    S.op("pool", lambda e: e.tensor_copy(out=w2rot[:, 0:8], in_=w2[0][0][:, 8:16]), reads=[w2[0][1]], writes=[bw2rot])
    S.op("pool", lambda e: e.tensor_copy(out=w2rot[:, 8:16], in_=w2[0][0][:, 0:8]), reads=[w2[0][1]], writes=[bw2rot])
    identb = S.sbuf("identb", [128, 128], BF16); bident = Buf()
    S.dma("sp", identb[:], ident_in[:, :], writes=[bident], sem="c_ident")
    kvalid = S.sbuf("kvalid", [128, LOC // 128], F32); bkvalid = Buf()
    S.dma("sp", kvalid[:], kvalid_in[:, :], writes=[bkvalid], sem="c_kvalid")
    onesb = S.sbuf("onesb", [128, NG, 64], BF16); bonesb = Buf()
    S.op("dve", lambda e: e.memset(onesb[:], 1.0), writes=[bonesb])
    stage(4)
    posb = S.sbuf("posb", [128, 2], F32); bposb = Buf()
    for t in ([] if '1' in SKIP else range(2)):
        ps, bps = P.pb[5]
        for j in range(32):
            S.op("pe", lambda e, t=t, j=j, ps=ps: e.matmul(ps[:, 0:1], lhsT=w1[t][0][:, j, :], rhs=posT[t][0][:, j:j + 1], start=(j == 0), stop=(j == 31)),
                 reads=[w1[t][1], posT[t][1]], writes=[bps], mark=(j == 31))
        S.op("dve", lambda e, t=t, ps=ps: e.tensor_copy(out=posb[:, t:t + 1], in_=ps[:, 0:1]), reads=[bps], writes=[bposb])
    stage(5)
    P.norm_bf16 = True
    P.sq_ring = Ring(S, "sq", 2, [128, 512], BF16)
    P.rs_ring = Ring(S, "rs", 2, [128, 512], F32)
    P.ht_ring = Ring(S, "ht", 2, [128, KC, 512], BF16)
    P.rp_ring = Ring(S, "rp", 4, [16, 512], F32)
    xs_ring = Ring(S, "xs", 2, [128, KC, 512], F32)
    posi_ring = Ring(S, "posi", 2, [16, 512], I32)
    posf = S.sbuf("posf", [16, 512], F32); bposf = Buf()
    CT = S.sbuf("CT", [16, 512], F32); ST = S.sbuf("ST", [16, 512], F32); bT = Buf()
    kst_ring = Ring(S, "kst", 4, [64, 512], BF16)
    R = S.sbuf("R", [64, NG * 2, 528], BF16); bR = [Buf() for _ in range(NG * 2)]
    S.op("pool", lambda e: e.memset(R[:], 0.0), writes=bR)
    vst_ring = Ring(S, "vst", 2, [128, NG, 4, 128], BF16)
    KCs = S.sbuf("KCs", [64, NG, 256], BF16); bKCs = Buf()
    VCT = S.sbuf("VCT", [64, NG, 256], BF16); bVCT = Buf()
    hid_ring = Ring(S, "hid", 2, [128, 32], F32)
    hidb_ring = Ring(S, "hidb", 2, [128, 32], BF16)
    prg = Ring.__new__(Ring); prg.slots = P.pb[0:4]; prg.i = 0
    prs = Ring.__new__(Ring); prs.slots = P.pb[4:6]; prs.i = 0
    GC = float(2 * np.sqrt(2 / np.pi))
    for tc in range(LOC // 512):
        c0 = tc * 512
        xs, bxs = xs_ring.next()
        slot = (xs_ring.i - 1) % 2
        for k in range(KC):
            S.dma("sp", xs[:, k, :], xloc[:, k, c0:c0 + 512], writes=[bxs], sem=f"xs{slot}")
        stage(6)
        pi_, bpi = posi_ring.next()
        S.dma("sp", pi_[:, :], pos16[:, c0:c0 + 512], writes=[bpi], sem=f"posi{slot}")
        S.op("dve", lambda e, pi_=pi_: e.tensor_copy(out=posf[:, :], in_=pi_[:, :]), reads=[bpi], writes=[bposf])
        stage(7)
        rope_tables(P, posf, bposf, 512, CT, ST, bT)
        stage(8)
        rs, brs = P.norm_stats(0, 512, src=xs, sb=[bxs])
        stage(8.5)
        ht, bht = P.ht_ring.next()
        P.norm_apply(ht, bht, 0, 0, 512, V_KV, rs, brs, src=xs, sb=[bxs])
        stage(9)
        for g in range(NG):
            for ti, ty in enumerate((2, 4)):
                if (ty == 4 and c0 < WK0) or '2' in SKIP:
                    continue
                col = (ty * NG + g) * HD
                pa, bpa = prg.next()
                for k in range(KC):
                    S.op("pe", lambda e, k=k, pa=pa, col=col, ht=ht: e.matmul(pa[0:64, :], lhsT=wkv[:, k, col:col + 64], rhs=ht[:, k, :], start=(k == 0), stop=(k == KC - 1)),
                         reads=[bwkv, bht], writes=[bpa], mark=(k == KC - 1))
                stage(9.1)
                DBG = os.environ.get("KV_DBG")
                pr, bpr = (prg.next() if DBG == "c" else prs.next())
                MM = 64 if DBG == "b" else 32
                for k in range(KC):
                    S.op("pe", lambda e, k=k, pr=pr, ti=ti, g=g, ht=ht, MM=MM, DBG=DBG: e.matmul(pr[0:MM, :], lhsT=(wkv[:, k, 0:MM] if DBG in ("a", "b", "d") else wrot[:, k, ti, g, :]), rhs=ht[:, k, :], start=(k == 0), stop=(k == KC - 1)),
                         reads=([bht] if DBG == 'd' else [bwrot, bht]), writes=[bpr], mark=(k == KC - 1))
                stage(9.2)
                kst, bkst = kst_ring.next()
                rope_evac(P, kst, bkst, 512, pa, bpa, pr, bpr, CT, ST, bT, slice(0, 512))
                stage(9.3)
                if ty == 2:
                    S.dma("sp", o_KS[g, :, c0:c0 + 512], kst[:, :], reads=[bkst], writes=[bout], sem="o_ks")
                else:
                    S.dma("sp", o_KW[g, :, c0 - WK0:c0 - WK0 + 512], kst[:, :], reads=[bkst], writes=[bout], sem="o_kw")
            for t in ([] if '3' in SKIP else range(2)):
                col = (t * NG + g) * HD
                ri = g * 2 + t
                pa, bpa = prg.next()
                for k in range(KC):
                    S.op("pe", lambda e, k=k, pa=pa, col=col, ht=ht: e.matmul(pa[0:64, :], lhsT=wkv[:, k, col:col + 64], rhs=ht[:, k, :], start=(k == 0), stop=(k == KC - 1)),
                         reads=[bwkv, bht], writes=[bpa], mark=(k == KC - 1))
                S.op("pool", lambda e, ri=ri: e.tensor_copy(out=R[:, ri, 0:16], in_=R[:, ri, 512:528]), reads=[bR[ri]], writes=[bR[ri]])
                S.op("act", lambda e, ri=ri, pa=pa: e.activation(out=R[:, ri, 16:528], in_=pa[0:64, :], func=AF.Identity), reads=[bpa], writes=[bR[ri]])
                ph, bph = prs.next()
                for j in range(32):
                    S.op("pe", lambda e, j=j, ph=ph, t=t, ri=ri: e.matmul(ph[:, 0:32], lhsT=w1[t][0][:, j, :], rhs=R[:, ri, j:j + 497:16], start=(j == 0), stop=(j == 31)),
                         reads=[w1[t][1], bR[ri]], writes=[bph], mark=(j == 31))
                z, bz = hid_ring.next()
                u, bu = hid_ring.next()
                S.op("act", lambda e, z=z, ph=ph, t=t: e.activation(out=z[:, :], in_=ph[:, 0:32], func=AF.Identity, bias=posb[:, t:t + 1]), reads=[bph, bposb], writes=[bz])
                S.op("dve", lambda e, z=z, u=u: e.tensor_tensor(out=u[:, :], in0=z[:, :], in1=z[:, :], op=ALU.mult), reads=[bz], writes=[bu])
                S.op("dve", lambda e, u=u: e.tensor_scalar(out=u[:, :], in0=u[:, :], scalar1=0.044715, scalar2=1.0, op0=ALU.mult, op1=ALU.add), reads=[bu], writes=[bu])
                S.op("dve", lambda e, z=z, u=u: e.tensor_tensor(out=u[:, :], in0=u[:, :], in1=z[:, :], op=ALU.mult), reads=[bu, bz], writes=[bu])
                S.op("act", lambda e, u=u: e.activation(out=u[:, :], in_=u[:, :], func=AF.Sigmoid, scale=GC), reads=[bu], writes=[bu])
                hb, bhb = hidb_ring.next()
                S.op("dve", lambda e, z=z, u=u, hb=hb: e.tensor_tensor(out=hb[:, :], in0=u[:, :], in1=z[:, :], op=ALU.mult), reads=[bu, bz], writes=[bhb])
                pk, bpk = prg.next()
                S.op("pe", lambda e, pk=pk, t=t, hb=hb: e.matmul(pk[0:64, 0:32], lhsT=w2[t][0][:, :], rhs=hb[:, :], start=True, stop=True),
                     reads=[w2[t][1], bhb], writes=[bpk], mark=True)
                if t == 0:
                    pr, bpr = prs.next()
                    S.op("pe", lambda e, pr=pr, hb=hb: e.matmul(pr[0:32, 0:32], lhsT=w2rot[:, :], rhs=hb[:, :], start=True, stop=True),
                         reads=[bw2rot, bhb], writes=[bpr], mark=True)
                    rope_evac(P, KCs[:, g, tc * 32:(tc + 1) * 32], bKCs, 32, pk, bpk, pr, bpr, CT, ST, bT, slice(15, 512, 16))
                else:
                    S.op("act", lambda e, pk=pk, g=g, tc=tc: e.activation(out=VCT[:, g, tc * 32:(tc + 1) * 32], in_=pk[0:64, 0:32], func=AF.Identity),
                         reads=[bpk], writes=[bVCT])
        stage(10)
        for ty in (3, 5):
            if (ty == 5 and c0 < WK0) or '4' in SKIP:
                continue
            vst, bvst = vst_ring.next()
            for tt in range(4):
                kt = tc * 4 + tt
                pv, bpv = prg.next()
                for k in range(KC):
                    S.op("pe", lambda e, k=k, pv=pv, tt=tt, ty=ty, ht=ht: e.matmul(pv[:, 0:256], lhsT=ht[:, k, tt * 128:(tt + 1) * 128], rhs=wkv[:, k, ty * 256:(ty + 1) * 256], start=(k == 0), stop=(k == KC - 1)),
                         reads=[bwkv, bht], writes=[bpv], mark=(k == KC - 1))
                S.op("dve", lambda e, pv=pv, vst=vst, tt=tt, kt=kt: e.tensor_scalar(out=vst[:, :, tt, 0:64], in0=pv[:, 0:256].rearrange("p (g d) -> p g d", g=NG),
                                                                                   scalar1=kvalid[:, kt:kt + 1], scalar2=None, op0=ALU.mult),
                     reads=[bpv, bkvalid], writes=[bvst])
                S.op("pool", lambda e, vst=vst, tt=tt, kt=kt: e.tensor_scalar(out=vst[:, :, tt, 64:128], in0=onesb[:, :, :], scalar1=kvalid[:, kt:kt + 1], scalar2=None, op0=ALU.mult),
                     reads=[bonesb, bkvalid], writes=[bvst])
            for g in range(NG):
                if ty == 3:
                    S.dma("sp", o_VS[g, :, tc * 4:(tc + 1) * 4, :], vst[:, g, :, :], reads=[bvst], writes=[bout], sem="o_vs")
                else:
                    k0 = (c0 - WK0) // 128
                    S.dma("sp", o_VW[g, :, k0:k0 + 4, :], vst[:, g, :, :], reads=[bvst], writes=[bout], sem="o_vw")
    for g in range(NG):
        S.dma("sp", o_KC[g, :, :], KCs[:, g, :], reads=[bKCs], writes=[bout], sem="o_kc")
    VCs = S.sbuf("VCs", [128, NG, 2, 128], BF16); bVCs = Buf()
    S.op("dve", lambda e: e.memset(VCs[:], 1.0), writes=[bVCs])
    pt_, bpt_ = P.pbt
    for g in ([] if '5' in SKIP else range(NG)):
        for ct in range(2):
            S.op("pe", lambda e, g=g, ct=ct: e.transpose(pt_[:, 0:64], VCT[:, g, ct * 128:(ct + 1) * 128], identb[0:64, 0:64]),
                 reads=[bVCT, bident], writes=[bpt_], mark=True)
            S.op("dve", lambda e, g=g, ct=ct: e.tensor_copy(out=VCs[:, g, ct, 0:64], in_=pt_[:, 0:64]), reads=[bpt_], writes=[bVCs])
    for g in range(NG):
        S.dma("sp", o_VC[g, :, :, :], VCs[:, g, :, :], reads=[bVCs], writes=[bout], sem="o_vc")
    S.wait_all("sp", [bout])
    S.emit()
    S.close()
    return P.nc


NEGB = -30000.0
NH = 16


def build_ATT(P):
    S = P.S
    ASTOP = float(os.environ.get('ATT_STOP', '99'))
    class _Stop(Exception):
        pass
    QSTOP = float(os.environ.get('ATT_QSTOP', '99'))
    cur_qc = [0]
    def stage(n):
        if (cur_qc[0] >= 1 and n < 5.4 and n > QSTOP) or n > ASTOP:
            raise _Stop()
    P.setup_common(stream=False)
    P.xT = S.sbuf("xT", [128, KC, OWN], F32)
    P.xbufs = [Buf(f"x{i}") for i in range(OWN // 64)]
    xin = P.din("xown", [128, KC, OWN])
    pos16 = P.din("pos16", [16, OWN], I32)
    win_in = P.din("w_in", [D, D + 48]).rearrange("(kc p) n -> p kc n", p=128)
    wout_in = P.din("w_out", [D, D]).rearrange("(h dd) n -> dd h n", dd=64)
    i_KS = P.din("KS", [NG, 64, LOC], BF16)
    i_KW = P.din("KW", [NG, 64, LOC - WK0], BF16)
    i_VS = P.din("VS", [NG, 128, LOC // 128, 128], BF16)
    i_VW = P.din("VW", [NG, 128, (LOC - WK0) // 128, 128], BF16)
    i_KC = P.din("KC", [NG, 64, 256], BF16)
    i_VC = P.din("VC", [NG, 128, 2, 128], BF16)
    ident_in = P.din("identb", [128, 128], BF16)
    expand_in = P.din("expand", [64, LOC], BF16)
    winb_in = P.din("winbias", [128, 8, 512], BF16)
    cmpb_in = P.din("cmpbias", [128, 2, OWN], BF16)
    selc_in = P.din("selc", [128, 16, 3, 64])
    ov_in = P.din("ov", [128, 2, 65], BF16)
    selg_in = P.din("selg", [48, 48, 64], BF16)
    xo = P.dout("xout", [128, KC, OWN])
    rope_alloc(P)
    def const(name, shape, dt, src, q="sp"):
        t = S.sbuf(name, shape, dt); b = Buf(name)
        S.dma(q, t[:], src, writes=[b], sem="c_" + name)
        return t, b
    identb, bident = const("identb_s", [128, 128], BF16, ident_in[:, :])
    expand, bexpand = const("expand_s", [64, LOC], BF16, expand_in[:, :])
    winb, bwinb = const("winb_s", [128, 8, 512], BF16, winb_in[:, :, :])
    ov, bov = const("ov_s", [128, 2, 65], BF16, ov_in[:, :, :])
    selg, bselg = const("selg_s", [48, 48, 64], BF16, selg_in[:, :, :])
    tiny = S.sbuf("tiny", [128, 1], F32); btiny = Buf()
    S.op("dve", lambda e: e.memset(tiny[:], 1e-30), writes=[btiny])
    wgt = S.sbuf("wgt", [128, KC, 64], BF16); bwgt = Buf()
    S.op("pool", lambda e: e.memset(wgt[:], 0.0), writes=[bwgt])
    S.dma("pool", wgt[:, :, 0:48], win_in[:, :, D:D + 48], writes=[bwgt], sem="wgt")
    wq_ring = Ring(S, "wq", 1, [128, KC, 256], BF16)
    wqrot = S.sbuf("wqrot", [128, KC, 4, 32], BF16); bwqrot = Buf()
    S.op("pool", lambda e: e.memset(wqrot[:], 0.0), writes=[bwqrot])
    for k in range(KC):
        S.dma("sp", P.xT[:, k, :], xin[:, k, :], writes=P.xbufs, sem="xin")
    P.norm_bf16 = True
    P.sq_ring = Ring(S, "sq", 2, [128, 512], BF16)
    P.rs_ring = Ring(S, "rs", 1, [128, 512], F32)
    P.ht_ring = Ring(S, "ht", 1, [128, KC, 512], BF16)
    P.rp_ring = Ring(S, "rp", 2, [16, 512], F32)
    posi_ring = Ring(S, "posi", 1, [16, 512], I32)
    posf = S.sbuf("posf", [16, 512], F32); bposf = Buf()
    CT = S.sbuf("CT", [16, 512], F32); ST = S.sbuf("ST", [16, 512], F32); bT = Buf()
    ks_ring = Ring(S, "ksg", 1, [64, LOC], BF16)
    vs_ring = Ring(S, "vsg", 1, [128, LOC // 128, 128], BF16)
    kw_ring = Ring(S, "kwg", 1, [64, 1024], BF16)
    vw_ring = Ring(S, "vwg", 1, [128, 8, 128], BF16)
    kc_ring = Ring(S, "kcg", 2, [64, 256], BF16)
    vc_ring = Ring(S, "vcg", 2, [128, 2, 128], BF16)
    cb_ring = Ring(S, "cmpb", 1, [128, 2, 512], BF16)
    Qt = S.sbuf("Qt", [64, 4, 512], BF16); bQ = [Buf() for _ in range(4)]
    gT = S.sbuf("gT", [48, 512], F32); bgT = Buf()
    gTh = S.sbuf("gTh", [48, 512], BF16); gTl = S.sbuf("gTl", [48, 512], BF16); bgHL = Buf()
    selc_ring = Ring(S, "selc", 1, [128, 4, 3, 64], F32)
    PcT = S.sbuf("PcT", [128, 4, 2, 512], BF16); bPc = [Buf() for _ in range(4)]
    pt_ring = Ring(S, "pT", 2, [128, 512], BF16)
    MT = S.sbuf("MT", [64, 512], BF16); bMT = Buf()
    Mpad = S.sbuf("Mpad", [128, 64], BF16); bMpad = Buf()
    acc = S.sbuf("acc", [64, 4, 512], F32); bacc = [Buf() for _ in range(4)]
    rsum_ring = Ring(S, "rsum", 1, [64, 512], F32)
    tn_ring = Ring(S, "tn", 1, [64, 512], F32)
    oT = S.sbuf("oT", [64, 4, 512], BF16); boT = Buf()
    wo_ring = Ring(S, "wo", 1, [64, 4, D], BF16)
    imp = S.sbuf("imp", [128, 64], F32); bimp = Buf()
    impc = S.sbuf("impc", [128, 64], F32); bimpc = Buf()
    imp2 = S.sbuf("imp2", [128, 64], F32); bimp2 = Buf()
    rc4 = S.sbuf("rc4", [128, 4], F32); brc4 = Buf()
    m8 = S.sbuf("m8", [128, 16], F32); bm8 = Buf()
    sring = Ring.__new__(Ring); sring.slots = P.pb[0:2]; sring.i = 0
    oring = Ring.__new__(Ring); oring.slots = P.pb[2:4]; oring.i = 0
    mring = Ring.__new__(Ring); mring.slots = P.pb[4:6]; mring.i = 0
    ptr, bptr = P.pbt
    layer_vec = P.layer_vec

    def finish(hh, h, br, po, bpo):
        rsm, brsm = rsum_ring.next()
        S.op("act", lambda e: e.activation(out=rsm[:, :], in_=po[64:128, :], func=AF.Identity),
             reads=[bpo, btiny], writes=[brsm])
        tn, btn = tn_ring.next()
        S.op("dve", lambda e: e.tensor_scalar(out=rsm[:, :], in0=rsm[:, :], scalar1=1e-30, scalar2=None, op0=ALU.max), reads=[brsm], writes=[brsm])
        S.op("dve", lambda e: e.reciprocal(out=rsm[:, :], in_=rsm[:, :]), reads=[brsm], writes=[brsm])
        S.op("dve", lambda e: e.tensor_tensor(out=tn[:, :], in0=po[0:64, :], in1=rsm[:, :], op=ALU.mult), reads=[bpo, brsm], writes=[btn])
        pg, bpg = mring.next()
        j = 3 * h + br
        S.op("pe", lambda e: e.matmul(pg[0:64, :], lhsT=selg[:, j, :], rhs=gTh[:, :], start=True, stop=False), reads=[bselg, bgHL], writes=[bpg], mark=False)
        S.op("pe", lambda e: e.matmul(pg[0:64, :], lhsT=selg[:, j, :], rhs=gTl[:, :], start=False, stop=True), reads=[bselg, bgHL], writes=[bpg], mark=True)
        if br == 0:
            S.op("dve", lambda e: e.tensor_tensor(out=acc[:, hh, :], in0=pg[0:64, :], in1=tn[:, :], op=ALU.mult), reads=[bpg, btn], writes=[bacc[hh]])
        else:
            S.op("dve", lambda e: e.tensor_tensor(out=tn[:, :], in0=pg[0:64, :], in1=tn[:, :], op=ALU.mult), reads=[bpg, btn], writes=[btn])
            S.op("pool", lambda e: e.tensor_tensor(out=acc[:, hh, :], in0=acc[:, hh, :], in1=tn[:, :], op=ALU.add), reads=[btn, bacc[hh]], writes=[bacc[hh]])

    def _att_loop():
        for qc in range(OWN // 512):
            c0 = 512 * qc
            cur_qc[0] = qc
            S.barrier()
            q0 = 512 * qc
            nkt = 16 + 4 * qc + 4
            pi_, bpi, sm = posi_ring.nexts()
            S.dma("sp", pi_[:, :], pos16[:, q0:q0 + 512], writes=[bpi], sem=sm)
            S.op("dve", lambda e, pi_=pi_: e.tensor_copy(out=posf[:, :], in_=pi_[:, :]), reads=[bpi], writes=[bposf])
            stage(0.1)
            rope_tables(P, posf, bposf, 512, CT, ST, bT)
            stage(0.2)
            rs, brs = P.norm_stats(c0, 512)
            stage(0.25)
            ht, bht = P.ht_ring.next()
            P.norm_apply(ht, bht, 0, c0, 512, layer_vec, rs, brs)
            cb, bcb, sm = cb_ring.nexts()
            S.dma("sp", cb[:, :, :], cmpb_in[:, :, q0:q0 + 512], writes=[bcb], sem=sm)
            stage(0.3)
            pgt, bpgt = P.pb[6]
            for k in range(KC):
                S.op("pe", lambda e, k=k, ht=ht: e.matmul(pgt[0:64, :], lhsT=wgt[:, k, :], rhs=ht[:, k, :], start=(k == 0), stop=(k == KC - 1)),
                     reads=[bwgt, bht], writes=[bpgt], mark=(k == KC - 1))
            S.op("act", lambda e: e.activation(out=gT[:, :], in_=pgt[0:48, :], func=AF.Sigmoid), reads=[bpgt], writes=[bgT])
            stage(0.4)
            S.op("dve", lambda e: e.tensor_copy(out=gTh[:, :], in_=gT[:, :]), reads=[bgT], writes=[bgHL])
            S.op("dve", lambda e: e.tensor_tensor(out=gT[:, :], in0=gT[:, :], in1=gTh[:, :], op=ALU.subtract), reads=[bgT, bgHL], writes=[bgT])
            S.op("dve", lambda e: e.tensor_copy(out=gTl[:, :], in_=gT[:, :]), reads=[bgT], writes=[bgHL])
            stage(0.45)
            selc, bselc, sem_ = selc_ring.nexts()
            S.dma("sp", selc[:, :, :, :], selc_in[:, 4 * qc:4 * qc + 4, :, :], writes=[bselc], sem=sem_)
            for g in range(NG):
                stage(1)
                ksg, bks, sm = ks_ring.nexts()
                S.dma("sp", ksg[:, 0:nkt * 128], i_KS[g, :, 0:nkt * 128], writes=[bks], sem=sm)
                vsg, bvs, sm = vs_ring.nexts()
                S.dma("sp", vsg[:, 0:nkt, :], i_VS[g, :, 0:nkt, :], writes=[bvs], sem=sm)
                kwg, bkw, sm = kw_ring.nexts()
                S.dma("sp", kwg[:, :], i_KW[g, :, 512 * qc:512 * qc + 1024], writes=[bkw], sem=sm)
                vwg, bvw, sm = vw_ring.nexts()
                S.dma("sp", vwg[:, :, :], i_VW[g, :, 4 * qc:4 * qc + 8, :], writes=[bvw], sem=sm)
                kcg, bkc, sm = kc_ring.nexts()
                S.dma("sp", kcg[:, :], i_KC[g, :, :], writes=[bkc], sem=sm)
                vcg, bvc, sm = vc_ring.nexts()
                S.dma("sp", vcg[:, :, :], i_VC[g, :, :, :], writes=[bvc], sem=sm)
                wq, bwq, sm = wq_ring.nexts()
                S.dma("pool", wq[:, :, :], win_in[:, :, 256 * g:256 * g + 256], writes=[bwq], sem=sm)
                vq = wq[:, :, :].rearrange("p k (h d) -> p k h d", h=4)
                S.op("pool", lambda e, vq=vq: e.tensor_copy(out=wqrot[:, :, :, 0:8], in_=vq[:, :, :, 8:16]), reads=[bwq], writes=[bwqrot])
                S.op("pool", lambda e, vq=vq: e.tensor_copy(out=wqrot[:, :, :, 8:16], in_=vq[:, :, :, 0:8]), reads=[bwq], writes=[bwqrot])
                for hh in range(4):
                    h = 4 * g + hh
                    pa, bpa = mring.next()
                    for k in range(KC):
                        S.op("pe", lambda e, k=k, pa=pa, hh=hh, ht=ht, wq=wq: e.matmul(pa[0:64, :], lhsT=wq[:, k, hh * 64:(hh + 1) * 64], rhs=ht[:, k, :], start=(k == 0), stop=(k == KC - 1)),
                             reads=[bwq, bht], writes=[bpa], mark=(k == KC - 1))
                    pr, bpr = mring.next()
                    for k in range(KC):
                        S.op("pe", lambda e, k=k, pr=pr, hh=hh, ht=ht: e.matmul(pr[0:32, :], lhsT=wqrot[:, k, hh, :], rhs=ht[:, k, :], start=(k == 0), stop=(k == KC - 1)),
                             reads=[bwqrot, bht], writes=[bpr], mark=(k == KC - 1))
                    rope_evac(P, Qt[:, hh, :], bQ[hh], 512, pa, bpa, pr, bpr, CT, ST, bT, slice(0, 512))
                    stage(2)
                for hh in range(4):
                    h = 4 * g + hh
                    for ct in range(2):
                        ps, bps = sring.next()
                        S.op("pe", lambda e, ps=ps, ct=ct, hh=hh, kcg=kcg: e.matmul(ps[:, :], lhsT=kcg[:, ct * 128:(ct + 1) * 128], rhs=Qt[:, hh, :], start=True, stop=False),
                             reads=[bkc, bQ[hh]], writes=[bps], mark=False)
                        S.op("pe", lambda e, ps=ps, ct=ct, cb=cb: e.matmul(ps[:, :], lhsT=identb[:, :], rhs=cb[:, ct, :], start=False, stop=True),
                             reads=[bident, bcb], writes=[bps], mark=True)
                        S.op("act", lambda e, ps=ps, ct=ct, hh=hh: e.activation(out=PcT[:, hh, ct, :], in_=ps[:, :], func=AF.Exp, scale=0.125),
                             reads=[bps], writes=[bPc[hh]])
                    po, bpo = oring.next()
                    for ct in range(2):
                        S.op("pe", lambda e, po=po, ct=ct, hh=hh, vcg=vcg: e.matmul(po[:, :], lhsT=vcg[:, ct, :], rhs=PcT[:, hh, ct, :], start=(ct == 0), stop=(ct == 1)),
                             reads=[bvc, bPc[hh]], writes=[bpo], mark=(ct == 1))
                    finish(hh, h, 0, po, bpo)
                stage(3)
                for qt in range(4):
                    pu, bpu = mring.next()
                    for hh in range(4):
                        for ct in range(2):
                            S.op("pe", lambda e, pu=pu, hh=hh, ct=ct, qt=qt: e.matmul(pu[:, hh * 65:(hh + 1) * 65], lhsT=PcT[:, hh, ct, qt * 128:(qt + 1) * 128], rhs=ov[:, ct, :], start=(ct == 0), stop=(ct == 1)),
                                 reads=[bPc[hh], bov], writes=[bpu], mark=(hh == 3 and ct == 1))
                    puv = pu[:, 0:260].rearrange("p (h c) -> p h c", h=4)
                    S.op("dve", lambda e, puv=puv: e.tensor_scalar(out=rc4[:, :], in0=puv[:, :, 64], scalar1=1e-30, scalar2=None, op0=ALU.add), reads=[bpu], writes=[brc4])
                    S.op("dve", lambda e: e.reciprocal(out=rc4[:, :], in_=rc4[:, :]), reads=[brc4], writes=[brc4])
                    S.op("dve", lambda e, puv=puv: e.tensor_scalar(out=imp[:, :], in0=puv[:, 0, 0:64], scalar1=rc4[:, 0:1], scalar2=None, op0=ALU.mult), reads=[bpu, brc4], writes=[bimp])
                    for hh in range(1, 4):
                        S.op("dve", lambda e, puv=puv, hh=hh: e.scalar_tensor_tensor(out=imp[:, :], in0=puv[:, hh, 0:64], scalar=rc4[:, hh:hh + 1], in1=imp[:, :], op0=ALU.mult, op1=ALU.add),
                             reads=[bpu, brc4, bimp], writes=[bimp])
                    qti = qt
                    S.op("dve", lambda e, qti=qti: e.tensor_tensor(out=impc[:, :], in0=imp[:, :], in1=selc[:, qti, 0, :], op=ALU.mult), reads=[bimp, bselc], writes=[bimpc])
                    S.op("dve", lambda e, qti=qti: e.tensor_tensor(out=impc[:, :], in0=impc[:, :], in1=selc[:, qti, 2, :], op=ALU.add), reads=[bimpc, bselc], writes=[bimpc])
                    S.op("dve", lambda e: e.max(out=m8[:, 0:8], in_=impc[:, :]), reads=[bimpc], writes=[bm8])
                    S.op("dve", lambda e: e.match_replace(out=imp2[:, :], in_to_replace=m8[:, 0:8], in_values=impc[:, :], imm_value=-2.0), reads=[bimpc, bm8], writes=[bimp2])
                    S.op("dve", lambda e: e.max(out=m8[:, 8:16], in_=imp2[:, :]), reads=[bimp2, bm8], writes=[bm8])
                    S.op("dve", lambda e: e.tensor_scalar(out=imp2[:, :], in0=impc[:, :], scalar1=m8[:, 12:13], scalar2=None, op0=ALU.is_ge), reads=[bimpc, bm8, bimp2], writes=[bimp2])
                    S.op("dve", lambda e, qti=qti: e.tensor_tensor(out=imp2[:, :], in0=imp2[:, :], in1=selc[:, qti, 0, :], op=ALU.mult), reads=[bimp2, bselc], writes=[bimp2])
                    S.op("dve", lambda e, qti=qti: e.tensor_tensor(out=imp2[:, :], in0=imp2[:, :], in1=selc[:, qti, 1, :], op=ALU.add), reads=[bimp2, bselc], writes=[bimp2])
                    S.op("dve", lambda e: e.tensor_scalar(out=Mpad[:, :], in0=imp2[:, :], scalar1=1.0, scalar2=-NEGB, op0=ALU.subtract, op1=ALU.mult), reads=[bimp2], writes=[bMpad])
                    S.op("pe", lambda e: e.transpose(ptr[0:64, 0:128], Mpad[:, :], identb[:, :]), reads=[bMpad, bident], writes=[bptr], mark=True)
                    S.op("dve", lambda e, qt=qt: e.tensor_copy(out=MT[:, qt * 128:(qt + 1) * 128], in_=ptr[0:64, 0:128]), reads=[bptr], writes=[bMT])
                stage(4)
                for hh in range(4):
                    h = 4 * g + hh
                    po, bpo = oring.next()
                    for kt in range(nkt):
                        diag = kt - (16 + 4 * qc)
                        ps, bps = sring.next()
                        S.op("pe", lambda e, ps=ps, kt=kt, hh=hh, ksg=ksg: e.matmul(ps[:, :], lhsT=ksg[:, kt * 128:(kt + 1) * 128], rhs=Qt[:, hh, :], start=True, stop=False),
                             reads=[bks, bQ[hh]], writes=[bps], mark=False)
                        S.op("pe", lambda e, ps=ps, kt=kt, diag=diag: e.matmul(ps[:, :], lhsT=expand[:, kt * 128:(kt + 1) * 128], rhs=MT[:, :], start=False, stop=(diag < 0)),
                             reads=[bexpand, bMT], writes=[bps], mark=(diag < 0))
                        if diag >= 0:
                            S.op("pe", lambda e, ps=ps, diag=diag: e.matmul(ps[:, :], lhsT=identb[:, :], rhs=winb[:, 4 + diag, :], start=False, stop=True),
                                 reads=[bident, bwinb], writes=[bps], mark=True)
                        pT_, bpT_ = pt_ring.next()
                        S.op("act", lambda e, ps=ps, pT_=pT_: e.activation(out=pT_[:, :], in_=ps[:, :], func=AF.Exp, scale=0.125), reads=[bps], writes=[bpT_])
                        S.op("pe", lambda e, po=po, kt=kt, pT_=pT_, vsg=vsg: e.matmul(po[:, :], lhsT=vsg[:, kt, :], rhs=pT_[:, :], start=(kt == 0), stop=(kt == nkt - 1)),
                             reads=[bvs, bpT_], writes=[bpo], mark=(kt == nkt - 1))
                    stage(4.5 if hh == 0 else (4.96 if hh == 1 else 4.996))
                    finish(hh, h, 1, po, bpo)
                    stage(4.7 if hh == 0 else 4.97)
                    po, bpo = oring.next()
                    for i in range(8):
                        ps, bps = sring.next()
                        S.op("pe", lambda e, ps=ps, i=i, hh=hh, kwg=kwg: e.matmul(ps[:, :], lhsT=kwg[:, i * 128:(i + 1) * 128], rhs=Qt[:, hh, :], start=True, stop=False),
                             reads=[bkw, bQ[hh]], writes=[bps], mark=False)
                        S.op("pe", lambda e, ps=ps, i=i: e.matmul(ps[:, :], lhsT=identb[:, :], rhs=winb[:, i, :], start=False, stop=True),
                             reads=[bident, bwinb], writes=[bps], mark=True)
                        pT_, bpT_ = pt_ring.next()
                        S.op("act", lambda e, ps=ps, pT_=pT_: e.activation(out=pT_[:, :], in_=ps[:, :], func=AF.Exp, scale=0.125), reads=[bps], writes=[bpT_])
                        S.op("pe", lambda e, po=po, i=i, pT_=pT_, vwg=vwg: e.matmul(po[:, :], lhsT=vwg[:, i, :], rhs=pT_[:, :], start=(i == 0), stop=(i == 7)),
                             reads=[bvw, bpT_], writes=[bpo], mark=(i == 7))
                    stage(4.8 if hh == 0 else 4.98)
                    finish(hh, h, 2, po, bpo)
                    stage(4.9 if hh == 0 else 4.99)
                    S.op("act", lambda e, hh=hh, h=h: e.activation(out=oT[:, hh, :], in_=acc[:, hh, :], func=AF.Identity), reads=[bacc[hh]], writes=[boT])
                    stage(4.95 if hh == 0 else (4.995 if hh == 1 else 4.9995))
                    S.barrier()
                stage(5)
                wo, bwo, sm = wo_ring.nexts()
                S.dma("pool", wo[:, :, :], wout_in[:, 4 * g:4 * g + 4, :], writes=[bwo], sem=sm)
                for d in range(KC):
                    pw, bpw = mring.next()
                    for hh in range(4):
                        S.op("pe", lambda e, pw=pw, hh=hh, wo=wo, d=d: e.matmul(pw[:, :], lhsT=wo[:, hh, d * 128:(d + 1) * 128], rhs=oT[:, hh, :], start=(hh == 0), stop=(hh == 3)),
                             reads=[bwo, boT], writes=[bpw], mark=(hh == 3))
                    S.op("dve", lambda e, pw=pw, d=d, c0=c0: e.tensor_tensor(out=P.xT[:, d, c0:c0 + 512], in0=pw[:, :], in1=P.xT[:, d, c0:c0 + 512], op=ALU.add),
                         reads=[bpw] + P.xB(c0, c0 + 512), writes=P.xB(c0, c0 + 512))
                stage(5.5 + 0.1 * g + qc)

    try:
      _att_loop()
    except _Stop:
      pass
    bo = Buf("xout")
    for k in range(KC):
        S.dma("sp", xo[:, k, :], P.xT[:, k, :], reads=P.xbufs, writes=[bo], sem="xout")
    S.wait_all("sp", [bo])
    S.emit()
    S.close()
    return P.nc


def fm(v):
    return np.ascontiguousarray(np.asarray(v, np.float32).reshape(KC, 128).T)


def make_vecs(inp):
    vecs = np.zeros((128, NVEC, KC), np.float32)
    for l in range(4):
        vecs[:, V_MIX + l] = fm(inp["norm_mix"][l])
        vecs[:, V_FFN + l] = fm(inp["norm_ffn"][l])
        vecs[:, V_PLE + l] = fm(inp["norm_ple"][l])
    for l in range(2):
        vecs[:, V_PSC + l] = fm(inp["pool_scale"][l])
    vecs[:, V_KV] = fm(inp["norm_kv"])
    vecs[:, V_FIN] = fm(inp["norm_final"])
    return vecs


def make_convp(inp):
    cw = np.asarray(inp["ffn_conv"], np.float32)
    cb = np.asarray(inp["ffn_conv_b"], np.float32)
    out = np.zeros((128, 4, 2 * NFC, 4), np.float32)
    for l in range(4):
        for j in range(3):
            out[:, l, :, j] = cw[l, j].reshape(2 * NFC, 128).T
        out[:, l, :, 3] = cb[l].reshape(2 * NFC, 128).T
    return out


def core_range(c):
    b, half = c // 2, c % 2
    return b, half * OWN


def prep_A(inp):
    x = np.asarray(inp["x"], np.float32)
    p = np.asarray(inp["p"], np.float32)
    vecs = make_vecs(inp)
    convp = make_convp(inp)
    maps = []
    for c in range(NCORES):
        b, t0 = core_range(c)
        xT0 = np.zeros((128, KC, XC), np.float32)
        pT = np.zeros((2, 128, 2, T), np.float32)
        lo = t0 - HALO
        s = max(lo, 0)
        xs = x[b, s:t0 + OWN]
        xT0[:, :, PAD + (s - lo):] = xs.T.reshape(KC, 128, -1).transpose(1, 0, 2)
        for l in range(2):
            ps = p[l, b, s:t0 + OWN]
            pT[l, :, :, (s - lo):] = ps.T.reshape(2, 128, -1).transpose(1, 0, 2)
        flag = np.full((128, 1), 1.0 if lo >= 0 else 0.0, np.float32)
        invcnt = np.zeros((128, 4, 2, 16), np.float32)
        for gi, win in enumerate(WINS):
            tt = t0 + np.arange(16)
            invcnt[:, gi, :, :] = (1.0 / np.minimum(tt + 1, win)).astype(np.float32)
        maps.append({"vecs": vecs, "convp": convp, "xT0": xT0, "pT": pT, "flag": flag, "invcnt": invcnt,
                     "ffn_up": np.asarray(inp["ffn_up"][:2], np.float32), "ffn_down": np.asarray(inp["ffn_down"][:2], np.float32),
                     "ple_gate": np.asarray(inp["ple_gate"][:2], np.float32), "ple_proj": np.asarray(inp["ple_proj"][:2], np.float32),
                     "pool_w": np.asarray(inp["pool_w"], np.float32)})
    return maps


_CACHE = {}


def run_A(inp, cores=None):
    if "A" not in _CACHE:
        _CACHE["A"] = Prog("A").build_A()
    maps = prep_A(inp)
    if cores is not None:
        maps = [maps[c] for c in cores]
    res = run_bass_kernel_spmd(_CACHE["A"], maps, core_ids=list(range(len(maps))))
    return [r["x2T"] for r in res.results]


def rope_consts():
    inv = (500000.0 ** (-np.arange(0, 16, 2, dtype=np.float32) / 16)).astype(np.float32)
    rc = np.zeros((16, 2), np.float32)
    rc[:, 0] = np.concatenate([inv, inv])
    rc[:8, 1] = -1.0
    rc[8:, 1] = 1.0
    return rc


def bf16(a):
    import ml_dtypes
    return np.asarray(a).astype(ml_dtypes.bfloat16)


def prep_KV(inp, x2T):
    pos = np.asarray(inp["positions"], np.int32)
    maps = []
    for c in range(NCORES):
        b, t0 = core_range(c)
        xloc = np.zeros((128, KC, LOC), np.float32)
        pos16 = np.zeros((16, LOC), np.int32)
        kvalid = np.zeros((128, LOC // 128), np.float32)
        xloc[:, :, OWN:] = x2T[c]
        pos16[:, OWN:] = pos[b, t0:t0 + OWN][None, :]
        kvalid[:, OWN // 128:] = 1.0
        if t0 > 0:
            xloc[:, :, :OWN] = x2T[c - 1]
            pos16[:, :OWN] = pos[b, t0 - OWN:t0][None, :]
            kvalid[:, :OWN // 128] = 1.0
        m = {"vecs": make_vecs(inp), "convp": make_convp(inp), "xloc": xloc, "pos16": pos16, "keyvalid": kvalid,
             "w_kv": np.asarray(inp["w_kv"], np.float32), "ropec": rope_consts(),
             "identb": bf16(np.eye(128, dtype=np.float32))}
        for t in "kv":
            m[f"cmp_w1_{t}"] = np.asarray(inp[f"cmp_w1_{t}"], np.float32)
            m[f"cmp_w2_{t}"] = np.asarray(inp[f"cmp_w2_{t}"], np.float32)
            m[f"cmp_posT_{t}"] = np.ascontiguousarray(np.asarray(inp[f"cmp_pos_{t}"], np.float32).T)
        maps.append(m)
    return maps


def run_KV(inp, x2T, cores=None):
    if "KV" not in _CACHE:
        _CACHE["KV"] = build_KV(Prog("KV"))
    maps = prep_KV(inp, x2T)
    if cores is not None:
        maps = [maps[c] for c in cores]
    res = run_bass_kernel_spmd(_CACHE["KV"], maps, core_ids=list(range(len(maps))))
    return res.results


def att_consts(t0):
    off_tok = t0 - OWN
    expand = np.zeros((64, LOC), np.float32)
    for n in range(64):
        expand[n, 64 * n:64 * n + 64] = 1.0
    p = np.arange(128)[:, None, None]; i = np.arange(8)[None, :, None]; qq = np.arange(512)[None, None, :]
    kr = 128 * i + p - 512
    winb = np.where((kr <= qq) & (qq - kr < 512), 0.0, NEGB).astype(np.float32)
    cpp = (np.arange(2)[None, :, None] * 128 + np.arange(128)[:, None, None])
    c_abs = cpp - 1 + off_tok // 16
    t_abs = t0 + np.arange(OWN)[None, None, :]
    okc = (c_abs >= 0) & (c_abs <= 254) & (16 * c_abs + 31 <= t_abs)
    cmpb = np.where(okc, 0.0, NEGB).astype(np.float32)
    t_q = t0 + (np.arange(16)[None, :] * 128 + np.arange(128)[:, None])
    cur = t_q // 64
    nb = np.arange(64)[None, None, :] + off_tok // 64
    cand = ((nb >= 1) & (nb <= cur[:, :, None] - 2)).astype(np.float32)
    forced = ((nb >= 0) & ((nb == 0) | (nb == cur[:, :, None]) | (nb == cur[:, :, None] - 1))).astype(np.float32)
    selc = np.stack([cand, forced, cand - 1.0], axis=2).astype(np.float32)
    ov = np.zeros((128, 2, 65), np.float32)
    for ct in range(2):
        for pp in range(128):
            c2 = ct * 128 + pp
            lo, hi = 16 * c2 - 16, 16 * c2 + 16
            for n in range(64):
                if lo < 64 * (n + 1) and hi > 64 * n:
                    ov[pp, ct, n] = 1.0
    ov[:, :, 64] = 1.0
    selg = np.zeros((48, 48, 64), np.float32)
    for j in range(48):
        selg[j, j, :] = 1.0
    return {"expand": bf16(expand), "winbias": bf16(winb), "cmpbias": bf16(cmpb), "selc": selc, "ov": bf16(ov),
            "selg": bf16(selg), "identb": bf16(np.eye(128, dtype=np.float32)), "ropec": rope_consts()}


def prep_ATT(inp, j, xown, kvres):
    pos = np.asarray(inp["positions"], np.int32)
    vecs, convp = make_vecs(inp), make_convp(inp)
    maps = []
    for c in range(NCORES):
        b, t0 = core_range(c)
        m = {"vecs": vecs, "convp": convp, "xown": xown[c], "pos16": np.broadcast_to(pos[b, t0:t0 + OWN][None, :], (16, OWN)).copy(),
             "w_in": np.asarray(inp["w_in_b"][j], np.float32), "w_out": np.asarray(inp["w_out_b"][j], np.float32)}
        m.update(att_consts(t0))
        for k in ("KS", "KW", "VS", "VW", "KC", "VC"):
            m[k] = kvres[c][k]
        maps.append(m)
    return maps


def run_ATT(inp, j, xown, kvres, cores=None):
    key = f"ATT{j}"
    if key not in _CACHE:
        P = Prog(key)
        P.layer_vec = V_MIX + 2 + j
        _CACHE[key] = build_ATT(P)
    maps = prep_ATT(inp, j, xown, kvres)
    if cores is not None:
        maps = [maps[c] for c in cores]
    res = run_bass_kernel_spmd(_CACHE[key], maps, core_ids=list(range(len(maps))))
    return [r["xout"] for r in res.results]


def build_F(P, layer, final):
    S = P.S
    P.wl = 0
    P.setup_common(stream=True)
    for nm, shp in (("ffn_up", [1, D, 2 * DFF]), ("ffn_down", [1, DFF, D]), ("ple_gate", [1, D, D]), ("ple_proj", [1, PLE, D])):
        P.din(nm, shp)
    xin = P.din("xT0", [128, KC, XC])
    pTin = P.din("pT", [128, 2, OWN])
    P.alloc_work(1092)
    for k in range(KC):
        S.dma("sp", P.xT[:, k, :], xin[:, k, :], writes=P.xbufs, sem="xin")
    o0 = PAD + HALO
    ffn_chunks = [(o0 + 510 * i, 510) for i in range(4)] + [(o0 + 2040, 8)]
    P.ffn(layer, ffn_chunks, [[0, 1], [2, 3, 4]])
    P.ple(layer, [(o0 + 512 * i, 512, 512 * i) for i in range(4)], pTin)
    xo = P.dout("xout", [128, KC, OWN])
    bo = Buf("xout")
    if not final:
        for k in range(KC):
            S.dma("sp", xo[:, k, :], P.xT[:, k, o0:XC], reads=P.xB(o0, XC), writes=[bo], sem="xout")
    else:
        for i in range(4):
            c0 = o0 + 512 * i
            rs, brs = P.norm_stats(c0, 512)
            for k in range(KC):
                tmp, btmp = P.cv_ring.next()
                S.op("dve", lambda e, k=k, tmp=tmp, c0=c0, rs=rs: e.scalar_tensor_tensor(
                    out=tmp[:, :], in0=P.xT[:, k, c0:c0 + 512], scalar=P.vecs[:, V_FIN, k:k + 1], in1=rs[:, :],
                    op0=ALU.mult, op1=ALU.mult), reads=P.xB(c0, c0 + 512) + [P.bvecs, brs], writes=[btmp])
                S.dma("sp", xo[:, k, 512 * i:512 * i + 512], tmp[:, :], reads=[btmp], writes=[bo], sem="xout")
    S.wait_all("sp", [bo])
    S.emit()
    S.close()
    return P.nc


def run_F(inp, layer, xown, final, cores=None):
    key = f"F{layer}"
    if key not in _CACHE:
        _CACHE[key] = build_F(Prog(key), layer, final)
    p = np.asarray(inp["p"], np.float32)
    vecs, convp = make_vecs(inp), make_convp(inp)
    maps = []
    for c in range(NCORES):
        b, t0 = core_range(c)
        xT0 = np.zeros((128, KC, XC), np.float32)
        xT0[:, :, PAD + HALO:] = xown[c]
        if t0 > 0:
            xT0[:, :, PAD:PAD + HALO] = xown[c - 1][:, :, OWN - HALO:]
        pT = np.ascontiguousarray(p[layer, b, t0:t0 + OWN].T.reshape(2, 128, OWN).transpose(1, 0, 2))
        maps.append({"vecs": vecs, "convp": convp, "xT0": xT0, "pT": pT,
                     "ffn_up": np.asarray(inp["ffn_up"][layer:layer + 1], np.float32),
                     "ffn_down": np.asarray(inp["ffn_down"][layer:layer + 1], np.float32),
                     "ple_gate": np.asarray(inp["ple_gate"][layer:layer + 1], np.float32),
                     "ple_proj": np.asarray(inp["ple_proj"][layer:layer + 1], np.float32)})
    if cores is not None:
        maps = [maps[c] for c in cores]
    res = run_bass_kernel_spmd(_CACHE[key], maps, core_ids=list(range(len(maps))))
    return [r["xout"] for r in res.results]


def kernel(**inputs):
    inp = {k: np.asarray(v) for k, v in inputs.items()}
    x = run_A(inp)
    kvres = run_KV(inp, x)
    for j in range(2):
        x = run_ATT(inp, j, x, kvres)
        x = run_F(inp, 2 + j, x, final=(j == 1))
    out = np.zeros((B, SEQ, D), np.float32)
    for c in range(NCORES):
        b, t0 = core_range(c)
        out[b, t0:t0 + OWN] = x[c].transpose(1, 0, 2).reshape(D, OWN).T
    return out
```
